# Optimizing a Trainium2 kernel written in Bass

```python
import math
import jax
import jax.numpy as jnp
from jax import lax
import numpy as np


D_MODEL = 1024
BATCH = 16
SEQ = 4096
DEPTH = 1

GRID_W = 64
CTX_LEN = 256
ATT_HEADS = D_MODEL // 256
ATT_QK_DIM = 64
ATT_V_DIM = 2 * ATT_QK_DIM
ATT_WIDTH = ATT_HEADS * ATT_V_DIM
ATT_QK_WIDTH = ATT_HEADS * 2 * ATT_QK_DIM
LRU_WIDTH = D_MODEL - ATT_WIDTH
LRU_BLOCKS = 8
LRU_BLOCK_DIM = LRU_WIDTH // LRU_BLOCKS
LRU_C = 8.0
CONV_W = 4
CONV_LEFT = CONV_W // 2
IN_WIDTH = 2 * ATT_QK_WIDTH + ATT_WIDTH + 2 * LRU_WIDTH
FFN_HIDDEN = int(math.ceil(8 * D_MODEL / 3 / 256)) * 256
N_MOD = 6
Q_BLOCK = 128
ROPE_BASE = 10000.0
ROPE_FREQS = ATT_QK_DIM // 4
ATT_SCALE = ATT_QK_DIM ** -0.5
DEEPNORM_ALPHA = (2 * DEPTH) ** 0.25
DEEPNORM_BETA = (8 * DEPTH) ** -0.25
LN_EPS = 1e-6
RMS_EPS = 1e-5

kernel_name = 'hybrid_diffattn_rglru_block'


def layer_norm(x, g=None, b=None):
    xf = x.astype(jnp.float32)
    mu = jnp.mean(xf, axis=-1, keepdims=True)
    var = jnp.mean(jnp.square(xf - mu), axis=-1, keepdims=True)
    y = (xf - mu) * lax.rsqrt(var + LN_EPS)
    if g is not None:
        y = y * g + b
    return y.astype(x.dtype)


def modulate(h, shift, scale):
    return h * (1 + scale) + shift


def axial_rope_tables(n_tokens):
    rows = n_tokens // GRID_W
    row = jnp.repeat(jnp.arange(rows, dtype=jnp.float32), GRID_W)
    col = jnp.tile(jnp.arange(GRID_W, dtype=jnp.float32), rows)
    inv_freq = ROPE_BASE ** (-jnp.arange(ROPE_FREQS, dtype=jnp.float32) / ROPE_FREQS)
    ang = jnp.stack([row[:, None] * inv_freq, col[:, None] * inv_freq], axis=1)
    return jnp.cos(ang), jnp.sin(ang)


def apply_rope(t, cos, sin):
    tr = t.astype(jnp.float32).reshape(t.shape[:-1] + (2, 2, ROPE_FREQS))
    t1, t2 = tr[..., 0, :], tr[..., 1, :]
    cs, sn = cos[:, None, None], sin[:, None, None]
    out = jnp.stack([t1 * cs - t2 * sn, t2 * cs + t1 * sn], axis=-2)
    return out.reshape(t.shape).astype(t.dtype)


def split_proj(p):
    B, S, _ = p.shape
    q, k, v, u, g = jnp.split(p, [ATT_QK_WIDTH, 2 * ATT_QK_WIDTH, 2 * ATT_QK_WIDTH + ATT_WIDTH,
                                  2 * ATT_QK_WIDTH + ATT_WIDTH + LRU_WIDTH], axis=-1)
    q = q.reshape(B, S, ATT_HEADS, 2, ATT_QK_DIM)
    k = k.reshape(B, S, ATT_HEADS, 2, ATT_QK_DIM)
    v = v.reshape(B, S, ATT_HEADS, ATT_V_DIM)
    return q, k, v, u, g


def diff_attention(q, k_all, v_all, lam):
    B, S, H = q.shape[:3]
    nb = S // Q_BLOCK
    qb = jnp.moveaxis(q.reshape(B, nb, Q_BLOCK, H, 2, ATT_QK_DIM), 1, 0)

    def one_block(q_blk):
        s = jnp.einsum('bqhmd,bkhmd->bmhqk', q_blk, k_all).astype(jnp.float32) * ATT_SCALE
        p = jax.nn.softmax(s, axis=-1)
        a = p[:, 0] - lam * p[:, 1]
        return jnp.einsum('bhqk,bkhd->bqhd', a.astype(v_all.dtype), v_all)

    o = lax.map(one_block, qb)
    return jnp.moveaxis(o, 0, 1).reshape(B, S, H, ATT_V_DIM)


def diff_head_norm(o, g, lam_init):
    B, S = o.shape[:2]
    of = o.astype(jnp.float32)
    y = of * lax.rsqrt(jnp.mean(jnp.square(of), axis=-1, keepdims=True) + RMS_EPS) * g * (1.0 - lam_init)
    return y.astype(o.dtype).reshape(B, S, ATT_WIDTH)


def centred_conv(u, w, b):
    S = u.shape[1]
    up = jnp.pad(u, ((0, 0), (CONV_LEFT, CONV_W - 1 - CONV_LEFT), (0, 0)))
    out = up[:, 0:S] * w[0]
    for j in range(1, CONV_W):
        out = out + up[:, j:j + S] * w[j]
    return out + b


def rglru_coeffs(u, w_gates, b_gates, lam):
    B, S, _ = u.shape
    ub = u.reshape(B, S, LRU_BLOCKS, LRU_BLOCK_DIM)
    z = jnp.einsum('bshi,ghij->gbshj', ub, w_gates) + b_gates[:, None, None]
    gates = jax.nn.sigmoid(z.astype(jnp.float32)).reshape(2, B, S, LRU_WIDTH)
    r, i = gates[0], gates[1]
    log_a = -LRU_C * r * jax.nn.softplus(-lam.astype(jnp.float32))
    a = jnp.exp(log_a)
    mult = jnp.sqrt(-jnp.expm1(2.0 * log_a))
    return a, mult * i * u.astype(jnp.float32)


def linear_scan(a, b, h0, reverse):
    def combine(e1, e2):
        a1, b1 = e1
        a2, b2 = e2
        return a1 * a2, a2 * b1 + b2
    a_cum, h = lax.associative_scan(combine, (a, b), axis=1, reverse=reverse)
    return h + a_cum * h0[:, None]


def mixing_sublayer(h, hc, cos, sin, w_in, lam_q, lam_k, lam_init, subln_g, conv_w, conv_b,
                    w_gates, b_gates, lru_lambda, w_out, ctx_out):
    q, k, v, u, g = split_proj(h @ w_in)
    qc, kc, vc, uc, gc = split_proj(hc @ w_in)
    q = apply_rope(q, cos, sin)
    k = apply_rope(k, cos, sin)
    k_all = jnp.concatenate([k, kc], axis=1)
    v_all = jnp.concatenate([v, vc], axis=1)
    lq = jnp.sum(lam_q.astype(jnp.float32) * lam_k.astype(jnp.float32), axis=-1)
    lam = jnp.exp(lq[0]) - jnp.exp(lq[1]) + lam_init
    o = diff_head_norm(diff_attention(q, k_all, v_all, lam), subln_g, lam_init)
    u_conv = centred_conv(u, conv_w, conv_b)
    uc_conv = centred_conv(uc, conv_w, conv_b)
    zero_state = jnp.zeros((uc.shape[0], LRU_WIDTH), jnp.float32)
    lat_dirs, ctx_dirs = [], []
    for d in range(2):
        rev = d == 1
        a_c, b_c = rglru_coeffs(uc_conv, w_gates[d], b_gates[d], lru_lambda[d])
        hs_c = linear_scan(a_c, b_c, zero_state, rev)
        state = hs_c[:, 0] if rev else hs_c[:, -1]
        a_l, b_l = rglru_coeffs(u_conv, w_gates[d], b_gates[d], lru_lambda[d])
        lat_dirs.append(linear_scan(a_l, b_l, state, rev))
        ctx_dirs.append(hs_c)
    y_lru = (lat_dirs[0] + lat_dirs[1]).astype(u.dtype) * jax.nn.gelu(g)
    y = jnp.concatenate([o, y_lru], axis=-1) @ w_out
    if not ctx_out:
        return y, None
    oc = diff_head_norm(diff_attention(qc, kc, vc, lam), subln_g, lam_init)
    yc_lru = (ctx_dirs[0] + ctx_dirs[1]).astype(uc.dtype) * jax.nn.gelu(gc)
    yc = jnp.concatenate([oc, yc_lru], axis=-1) @ w_out
    return y, yc


def swiglu(h, w1, w2):
    a, b = jnp.split(h @ w1, 2, axis=-1)
    return (jax.nn.silu(a) * b) @ w2


def setup_inputs(seed: int = 0) -> dict:
    key = jax.random.key(seed)
    ks = jax.random.split(key, 22)

    def nrm(k, shape, s):
        return jax.random.normal(k, shape, jnp.float32) * s

    u = jax.random.uniform(ks[14], (DEPTH, 2, LRU_WIDTH), jnp.float32, minval=0.9, maxval=0.999)
    a0 = u ** (1.0 / LRU_C)
    lru_lambda = jnp.log(a0) - jnp.log1p(-a0)
    return {
        'x': nrm(ks[0], (BATCH, SEQ, D_MODEL), 1.0),
        'c': nrm(ks[1], (BATCH, D_MODEL), 1.0),
        'ctx': nrm(ks[2], (BATCH, CTX_LEN, D_MODEL), 1.0),
        'c_ctx': nrm(ks[3], (D_MODEL,), 1.0),
        'w_ada': nrm(ks[4], (DEPTH, D_MODEL, N_MOD * D_MODEL), 0.5 * D_MODEL ** -0.5),
        'b_ada': nrm(ks[5], (DEPTH, N_MOD * D_MODEL), 0.02),
        'w_in': nrm(ks[6], (DEPTH, D_MODEL, IN_WIDTH), D_MODEL ** -0.5),
        'lam_q': nrm(ks[7], (DEPTH, 2, ATT_QK_DIM), 0.1),
        'lam_k': nrm(ks[8], (DEPTH, 2, ATT_QK_DIM), 0.1),
        'subln_g': 1.0 + nrm(ks[9], (DEPTH, ATT_V_DIM), 0.02),
        'conv_w': nrm(ks[10], (DEPTH, CONV_W, LRU_WIDTH), CONV_W ** -0.5),
        'conv_b': nrm(ks[11], (DEPTH, LRU_WIDTH), 0.02),
        'lru_w_gates': nrm(ks[12], (DEPTH, 2, 2, LRU_BLOCKS, LRU_BLOCK_DIM, LRU_BLOCK_DIM), LRU_BLOCK_DIM ** -0.5),
        'lru_b_gates': nrm(ks[13], (DEPTH, 2, 2, LRU_BLOCKS, LRU_BLOCK_DIM), 0.02),
        'lru_lambda': lru_lambda,
        'w_out': nrm(ks[15], (DEPTH, D_MODEL, D_MODEL), DEEPNORM_BETA * D_MODEL ** -0.5),
        'ln1_g': 1.0 + nrm(ks[16], (DEPTH, D_MODEL), 0.02),
        'ln1_b': nrm(ks[17], (DEPTH, D_MODEL), 0.02),
        'w_ffn_in': nrm(ks[18], (DEPTH, D_MODEL, 2 * FFN_HIDDEN), D_MODEL ** -0.5),
        'w_ffn_out': nrm(ks[19], (DEPTH, FFN_HIDDEN, D_MODEL), DEEPNORM_BETA * FFN_HIDDEN ** -0.5),
        'ln2_g': 1.0 + nrm(ks[20], (DEPTH, D_MODEL), 0.02),
        'ln2_b': nrm(ks[21], (DEPTH, D_MODEL), 0.02),
    }


def reference(x, c, ctx, c_ctx, w_ada, b_ada, w_in, lam_q, lam_k, subln_g, conv_w, conv_b,
              lru_w_gates, lru_b_gates, lru_lambda, w_out, ln1_g, ln1_b, w_ffn_in, w_ffn_out,
              ln2_g, ln2_b):
    S = x.shape[1]
    cos, sin = axial_rope_tables(S)
    xc = ctx
    for l in range(DEPTH):
        last = l == DEPTH - 1
        lam_init = 0.8 - 0.6 * math.exp(-0.3 * l)
        mod = jax.nn.silu(c) @ w_ada[l] + b_ada[l]
        mod_c = jax.nn.silu(c_ctx) @ w_ada[l] + b_ada[l]
        sh1, sc1, g1, sh2, sc2, g2 = jnp.split(mod[:, None, :], N_MOD, axis=-1)
        csh1, csc1, cg1, csh2, csc2, cg2 = jnp.split(mod_c, N_MOD, axis=-1)
        y, yc = mixing_sublayer(modulate(layer_norm(x), sh1, sc1), modulate(layer_norm(xc), csh1, csc1),
                                cos, sin, w_in[l], lam_q[l], lam_k[l], lam_init, subln_g[l],
                                conv_w[l], conv_b[l], lru_w_gates[l], lru_b_gates[l], lru_lambda[l],
                                w_out[l], not last)
        x = layer_norm(DEEPNORM_ALPHA * x + g1 * y, ln1_g[l], ln1_b[l])
        f = swiglu(modulate(layer_norm(x), sh2, sc2), w_ffn_in[l], w_ffn_out[l])
        x = layer_norm(DEEPNORM_ALPHA * x + g2 * f, ln2_g[l], ln2_b[l])
        if not last:
            xc = layer_norm(DEEPNORM_ALPHA * xc + cg1 * yc, ln1_g[l], ln1_b[l])
            fc = swiglu(modulate(layer_norm(xc), csh2, csc2), w_ffn_in[l], w_ffn_out[l])
            xc = layer_norm(DEEPNORM_ALPHA * xc + cg2 * fc, ln2_g[l], ln2_b[l])
    return x
```

```python
import os
import numpy as np
from contextlib import ExitStack
import concourse.bass as bass
import concourse.mybir as mybir
from concourse.bass_utils import run_bass_kernel_spmd
import ml_dtypes

F32 = mybir.dt.float32
BF16 = mybir.dt.bfloat16
AF = mybir.ActivationFunctionType
ALU = mybir.AluOpType
AX = mybir.AxisListType

D = 1024
S = 4096
CL = 256
T = S + CL
NB = 2
NCORES = 8
FH = 2816
ALPHA = float(2.0 ** 0.25)
LN_EPS = 1e-6
RMS_EPS = 1e-5
WST = 2048
ROPE_ENG = os.environ.get("K_ROPE", "pool")


class Op:
    __slots__ = ("eng", "fn", "r", "w", "dma", "safe", "waits", "marked", "tok", "snap", "bar")

    def __init__(self, eng, fn, r, w, dma, safe, bar=False):
        self.eng = eng; self.fn = fn; self.r = r; self.w = w; self.dma = dma; self.safe = safe
        self.waits = (); self.marked = False; self.tok = None; self.snap = None; self.bar = bar


class Prog:
    ENG = ("pe", "act", "dve", "pool", "sync")
    KD = 12

    def __init__(self, nc, es):
        self.nc = nc
        self.h = {"pe": nc.tensor, "act": nc.scalar, "dve": nc.vector, "pool": nc.gpsimd, "sync": nc.sync}
        self.sem = {e: es.enter_context(nc.semaphore("s_" + e)) for e in self.ENG}
        self.cnt = {e: 0 for e in self.ENG}
        self.dsem = {e: [es.enter_context(nc.semaphore("d_%s_%d" % (e, i))) for i in range(self.KD)]
                     for e in ("sync", "pool")}
        self.dcnt = {e: 0 for e in ("sync", "pool")}
        self.ops = []
        self.deferred = []
        self.start = 0
        self.last_w = {}
        self.readers = {}
        self.known = {e: {} for e in self.ENG}
        self.known_dma = {e: set() for e in self.ENG}
        self.last_real = {e: None for e in self.ENG}
        self.dma_out = []

    def add(self, eng, fn, r=(), w=(), dma=False, safe=False):
        self.ops.append(Op(eng, fn, tuple(r), tuple(w), dma, safe))

    def dma(self, eng, out, in_, r=(), w=()):
        if eng == "pool":
            eng = "sync"
        self.add(eng, lambda h: h.dma_start(out=out, in_=in_), r, w, dma=True)

    def dma_defer(self, eng, out, in_, r=(), w=()):
        if eng == "pool":
            eng = "sync"
        self.deferred.append(Op(eng, lambda h: h.dma_start(out=out, in_=in_), tuple(r), tuple(w), True, False))

    def undefer(self):
        self.ops.extend(self.deferred)
        self.deferred = []

    def barrier(self):
        self.undefer()
        for e in self.ENG:
            self.ops.append(Op(e, None, (), (), False, False, bar=True))

    def flush(self):
        ops = self.ops
        for i in range(self.start, len(ops)):
            op = ops[i]
            E = op.eng
            deps = set()
            if op.bar:
                for F in self.ENG:
                    if self.last_real[F] is not None:
                        deps.add(self.last_real[F])
                deps.update(self.dma_out)
            for k in op.r:
                j = self.last_w.get(k)
                if j is not None:
                    deps.add(j)
            for k in op.w:
                j = self.last_w.get(k)
                if j is not None:
                    deps.add(j)
                deps.update(self.readers.get(k, ()))
            deps.discard(i)
            wc = {}
            wd = []
            kn = self.known[E]
            for j in deps:
                pj = ops[j]
                if pj.dma:
                    if j not in self.known_dma[E]:
                        wd.append(j)
                else:
                    F = pj.eng
                    if F == E and (E == "pe" or op.safe) and not op.bar:
                        continue
                    if kn.get(F, -1) >= j:
                        continue
                    if wc.get(F, -1) < j:
                        wc[F] = j
            for F, j in wc.items():
                ops[j].marked = True
                if kn.get(F, -1) < j:
                    kn[F] = j
            for j in list(wc.values()) + wd:
                sn = ops[j].snap
                if sn:
                    for G, v in sn.items():
                        if kn.get(G, -1) < v:
                            kn[G] = v
            for j in wd:
                self.known_dma[E].add(j)
            op.waits = list(wc.values()) + wd
            op.snap = dict(kn)
            if op.fn is not None:
                for k in op.r:
                    self.readers.setdefault(k, []).append(i)
                for k in op.w:
                    self.last_w[k] = i
                    self.readers[k] = []
                if op.dma:
                    self.dma_out.append(i)
                else:
                    self.last_real[E] = i
        for i in range(self.start, len(ops)):
            op = ops[i]
            E = op.eng
            h = self.h[E]
            for j in op.waits:
                pj = ops[j]
                assert pj.tok is not None, (i, j, op.eng, pj.eng, op.r, op.w, pj.r, pj.w)
                if pj.dma:
                    h.wait_ge(pj.tok[0], pj.tok[1])
                else:
                    h.wait_ge(self.sem[pj.eng], pj.tok)
            if op.fn is None:
                continue
            if op.dma:
                n = self.dcnt[E]
                sem = self.dsem[E][n % self.KD]
                val = 16 * (n // self.KD + 1)
                if n >= self.KD:
                    h.wait_ge(sem, val - 16)
                ins = op.fn(h)
                ins.then_inc(sem, 16)
                op.tok = (sem, val)
                self.dcnt[E] = n + 1
            else:
                ins = op.fn(h)
                if op.marked:
                    self.cnt[E] += 1
                    ins.then_inc(self.sem[E], 1)
                    op.tok = self.cnt[E]
            op.fn = None
        if any(o.bar for o in ops[self.start:]):
            for E in self.ENG:
                for F in self.ENG:
                    lr = self.last_real[F]
                    if lr is not None:
                        if self.known[E].get(F, -1) < lr:
                            self.known[E][F] = lr
                self.known_dma[E] = set()
            self.dma_out = []
        self.start = len(ops)


def build(stage=99, dbg=False):
    nc = bass.Bass("TRN2", target_bir_lowering=False)

    def din(name, shape, dt=F32):
        return nc.dram_tensor(name, shape, dt, kind="ExternalInput").ap()

    x = din("x", [NB, S, D]); ctx = din("ctx", [NB, CL, D]); cT = din("cT", [128, 8, 3])
    w_ada = din("w_ada", [D, 6144]); b_adaT = din("b_adaT", [128, 48]); b_ada = din("b_ada", [1, 6144])
    w_in = din("w_in", [D, 2560]); lamqk = din("lamqk", [2, 128]); subg_d = din("subg", [1, 128])
    conv_wT = din("conv_wT", [128, 4, 4]); conv_bT = din("conv_bT", [128, 4]); gate_bT = din("gate_bT", [128, 16])
    lru_lamT = din("lru_lamT", [128, 8]); wg_bd = din("wg_bd", [16, 128, 128])
    w_out = din("w_out", [D, D]); ln1 = din("ln1", [2, D]); w_f1 = din("w_f1", [D, 2 * FH])
    w_f2 = din("w_f2", [FH, D]); ln2 = din("ln2", [2, D])
    ident_d = din("ident", [128, 128], BF16); RT_d = din("RT", [128, 128], BF16)
    cosT = din("cosT", [128, S]); sinT = din("sinT", [128, S])
    out = nc.dram_tensor("out", [NB, S, D], F32, kind="ExternalOutput").ap()
    dbg_out = {}
    if dbg:
        for nm, shp, dt_ in (("d_mod", [128, 144], F32), ("d_k", [128, 4 * T], BF16), ("d_v", [128, 34 * 4 * 129], BF16), ("d_o", [128, 4 * S], BF16),
                             ("d_y", [128, 4 * S], BF16), ("d_x1", [S, D], F32)):
            dbg_out[nm] = nc.dram_tensor(nm, shp, dt_, kind="ExternalOutput").ap()

    uT_s = nc.dram_tensor("uT_s", [NB, 4, 128, T], F32).ap()
    gT_s = nc.dram_tensor("gT_s", [NB, 4, 128, S], F32).ap()
    hf_s = nc.dram_tensor("hf_s", [NB, 4, 128, S], F32).ap()
    qT_s = nc.dram_tensor("qT_s", [NB, 128, 4, S], BF16).ap()
    x1_s = nc.dram_tensor("x1_s", [NB * S, D], F32).ap()
    oT_s = nc.dram_tensor("oT_s", [NB, 128, 4, S], BF16).ap()
    yT_s = nc.dram_tensor("yT_s", [NB, 128, 4, S], BF16).ap()

    w_ada_v = w_ada.rearrange("(k p) n -> p k n", p=128)
    w_in_v = w_in.rearrange("(k p) n -> p k n", p=128)
    w_out_v = w_out.rearrange("(k p) n -> p k n", p=128)
    w_f1_v = w_f1.rearrange("(k p) n -> p k n", p=128)
    w_f2_v = w_f2.rearrange("(j p) n -> p j n", p=128)

    with ExitStack() as g:
        P = Prog(nc, g)

        uid = [0]

        SB_TOP = 229344
        SB_LINE = 196608

        def A(es, name, shape, dt=F32):
            uid[0] += 1
            nbytes = int(np.prod(shape[1:])) * (2 if dt == BF16 else 4)
            off = SB_TOP - nc.sbuf_bytes_remaining
            if False and off < SB_LINE < off + nbytes + 64:
                padb = SB_LINE - off
                es.enter_context(nc.sbuf_tensor("pad%d" % uid[0], [128, (padb + 3) // 4], F32))
            return es.enter_context(nc.sbuf_tensor("sb%d_%s" % (uid[0], name), shape, dt))

        def PS(es, name, shape, dt=F32):
            uid[0] += 1
            return es.enter_context(nc.psum_tensor("ps%d_%s" % (uid[0], name), shape, dt))

        def mm(out_, lhsT, rhs, start, stop, r, w, skip=False):
            if skip:
                P.add("pe", lambda h: h.matmul(out_, lhsT=lhsT, rhs=rhs, start=start, stop=stop, skip_group_check=True), r, w)
            else:
                P.add("pe", lambda h: h.matmul(out_, lhsT=lhsT, rhs=rhs, start=start, stop=stop), r, w)

        def tr(out_, in_, r, w):
            P.add("pe", lambda h: h.transpose(out=out_, in_=in_, identity=ident[:]), r, w)

        def act(out_, in_, func, r, w, bias=None, scale=None, accum=None, safe=False):
            kw = {}
            if bias is not None: kw["bias"] = bias
            if scale is not None: kw["scale"] = scale
            if accum is not None: kw["accum_out"] = accum
            P.add("act", lambda h: h.activation(out=out_, in_=in_, func=func, **kw), r, w, safe=safe)

        def ts(eng, out_, in0, s1, s2, op0, op1, r, w, safe=False):
            if s2 is None:
                P.add(eng, lambda h: h.tensor_scalar(out=out_, in0=in0, scalar1=s1, scalar2=None, op0=op0), r, w, safe=safe)
            else:
                P.add(eng, lambda h: h.tensor_scalar(out=out_, in0=in0, scalar1=s1, scalar2=s2, op0=op0, op1=op1), r, w, safe=safe)

        def tt(eng, out_, in0, in1, op, r, w, safe=False):
            P.add(eng, lambda h: h.tensor_tensor(out=out_, in0=in0, in1=in1, op=op), r, w, safe=safe)

        def stt(out_, in0, sc, in1, op0, op1, r, w, safe=False):
            P.add("dve", lambda h: h.scalar_tensor_tensor(out=out_, in0=in0, scalar=sc, in1=in1, op0=op0, op1=op1), r, w, safe=safe)

        def cp(eng, out_, in_, r, w, safe=False):
            if eng == "act":
                P.add("act", lambda h: h.activation(out=out_, in_=in_, func=AF.Copy), r, w, safe=safe)
            else:
                P.add(eng, lambda h: h.tensor_copy(out=out_, in_=in_), r, w, safe=safe)

        def recip(out_, in_, r, w):
            P.add("dve", lambda h: h.reciprocal(out=out_, in_=in_), r, w)

        ident = A(g, "ident", [128, 128], BF16)
        modT = A(g, "modT", [128, 48, 3])
        sc1p = A(g, "sc1p", [128, 8, 3])
        sc2p = A(g, "sc2p", [128, 8, 3])
        g2bc = A(g, "g2bc", [128, NB, 1024])
        epsln = A(g, "epsln", [128, 1])
        epsrms = A(g, "epsrms", [128, 1])
        gm = ExitStack()
        RTm = A(gm, "RTm", [128, 128], BF16)
        g1bc = A(gm, "g1bc", [128, NB, 1024])
        lam_neg = A(gm, "lam_neg", [128, 1])
        subg = A(gm, "subg", [128, 128])
        cw = A(gm, "cw", [128, 4, 4])
        cb = A(gm, "cb", [128, 4])
        gb = A(gm, "gb", [128, 16])
        c1 = A(gm, "c1", [128, 8])
        c1x2 = A(gm, "c1x2", [128, 8])
        one_t = A(gm, "one_t", [128, 1])
        wg = A(gm, "wg", [128, 16, 128], BF16)
        Win = A(gm, "Win", [128, 8, 2560], BF16)
        wst = [None, None]
        wst_n = [0]

        P.add("dve", lambda h: h.memset(epsln[:], LN_EPS), (), ["epsln"])
        P.add("dve", lambda h: h.memset(epsrms[:], RMS_EPS), (), ["epsrms"])

        def load_cast(dst_fn, src_fn, nchunks, key, shape3, stg=None, skey="wst", engs=("pool", "dve", "pool", "act")):
            stg = stg or wst
            for n in range(nchunks):
                sl = wst_n[0] % len(stg)
                wst_n[0] += 1
                st = stg[sl][:, 0:shape3[0] * shape3[1]].rearrange("p (a b) -> p a b", a=shape3[0])
                kk = skey[sl] if isinstance(skey, list) else (skey, sl)
                P.dma("sync", st, src_fn(n), (), [kk])
                eng = engs[n % len(engs)]
                cp(eng, dst_fn(n), st, [kk], [(key, n)])

        with ExitStack() as p0:
            wst[0] = A(p0, "wst0", [128, WST]); wst[1] = A(p0, "wst1", [128, WST])
            cTt = A(p0, "cTt", [128, 8, 3])
            scT = A(p0, "scT", [128, 8, 3])
            scb = A(p0, "scb", [128, 8, NB, 128])
            badT = A(p0, "badT", [128, 48])
            bst = [A(p0, "bst%d" % i, [128, 512]) for i in range(2)]
            lamt = A(p0, "lamt", [128, 2, 128])
            lprod = A(p0, "lprod", [128, 128])
            lq = A(p0, "lq", [128, 2])
            le = A(p0, "le", [128, 2])
            ltmp = A(p0, "ltmp", [128, 1])
            llam = A(p0, "llam", [128, 8])
            lz = A(p0, "lz", [128, 8]); lz2 = A(p0, "lz2", [128, 8]); lsr = A(p0, "lsr", [128, 8])
            wgst = A(p0, "wgst", [128, 16, 128])
            psmod = PS(p0, "psmod", [128, 48, 3])
            psg = [PS(p0, "psg%d" % i, [128, 512]) for i in range(2)]

            P.dma("sync", ident[:], ident_d[:, :], (), ["ident"])
            P.dma("sync", RTm[:], RT_d[:, :], (), ["RTm"])
            P.dma("sync", cTt[:], cT[:, :, :], (), ["cTt"])
            P.dma("sync", badT[:], b_adaT[:, :], (), ["badT"])
            P.dma("sync", lamt[:, 0, :], lamqk[0:1, :].to_broadcast([128, 128]), (), ["lamt0"])
            P.dma("sync", lamt[:, 1, :], lamqk[1:2, :].to_broadcast([128, 128]), (), ["lamt1"])
            P.dma("sync", subg[:], subg_d[0:1, :].to_broadcast([128, 128]), (), ["subg"])
            P.dma("sync", cw[:], conv_wT[:, :, :], (), ["cw"])
            P.dma("sync", cb[:], conv_bT[:, :], (), ["cb"])
            P.dma("sync", gb[:], gate_bT[:, :], (), ["gb"])
            P.dma("sync", llam[:], lru_lamT[:, :], (), ["llam"])
            P.dma("sync", wgst[:], wg_bd.rearrange("t p n -> p t n"), (), ["wgst"])
            cp("pool", wg[:], wgst[:], ["wgst"], ["wg"])
            act(scT[:], cTt[:], AF.Sigmoid, ["cTt"], ["scT"])
            tt("dve", scT[:], scT[:], cTt[:], ALU.mult, ["scT", "cTt"], ["scT"])
            for k in range(8):
                for b in range(NB):
                    cp("dve", scb[:, k, b, :], scT[:, k, b:b + 1].to_broadcast([128, 128]), ["scT"], [("scb", k, b)])
            tt("dve", lprod[:], lamt[:, 0, :], lamt[:, 1, :], ALU.mult, ["lamt0", "lamt1"], ["lprod"])
            P.add("dve", lambda h: h.tensor_reduce(out=lq[:], in_=lprod[:].rearrange("p (a b) -> p a b", a=2),
                                                  axis=AX.X, op=ALU.add), ["lprod"], ["lq"])
            act(le[:], lq[:], AF.Exp, ["lq"], ["le"])
            tt("dve", ltmp[:], le[:, 1:2], le[:, 0:1], ALU.subtract, ["le"], ["ltmp"])
            ts("dve", lam_neg[:], ltmp[:], -0.2, None, ALU.add, None, ["ltmp"], ["lam_neg"])
            ts("dve", subg[:], subg[:], 0.8, None, ALU.mult, None, ["subg"], ["subg"])
            act(lz[:], llam[:], AF.Exp, ["llam"], ["lz"], scale=-1.0)
            ts("dve", lz2[:], lz[:], 2.0, None, ALU.add, None, ["lz"], ["lz2"])
            recip(lz2[:], lz2[:], ["lz2"], ["lz2"])
            tt("dve", lz[:], lz[:], lz2[:], ALU.mult, ["lz", "lz2"], ["lz"])
            tt("dve", lz2[:], lz[:], lz[:], ALU.mult, ["lz"], ["lz2"])
            ts("dve", lsr[:], lz2[:], 1.0 / 11.0, 1.0 / 9.0, ALU.mult, ALU.add, ["lz2"], ["lsr"])
            for cf in (1.0 / 7.0, 1.0 / 5.0, 1.0 / 3.0, 1.0):
                tt("dve", lsr[:], lsr[:], lz2[:], ALU.mult, ["lsr", "lz2"], ["lsr"])
                ts("dve", lsr[:], lsr[:], cf, None, ALU.add, None, ["lsr"], ["lsr"])
            tt("dve", lsr[:], lsr[:], lz[:], ALU.mult, ["lsr", "lz"], ["lsr"])
            ts("dve", c1[:], lsr[:], -16.0, None, ALU.mult, None, ["lsr"], ["c1"])
            ts("dve", c1x2[:], lsr[:], -32.0, None, ALU.mult, None, ["lsr"], ["c1x2"])
            P.add("dve", lambda h: h.memset(one_t[:], 1.0), (), ["one_t"])

            for n in range(24):
                sl = wst_n[0] % 2
                wst_n[0] += 1
                wa = wst[sl][:, :].rearrange("p (k c) -> p k c", k=8)
                P.dma("sync", wa, w_ada_v[:, :, n * 256:(n + 1) * 256], (), [("wst", sl)])
                for j in range(2):
                    fc = 2 * n + j
                    if 16 <= fc < 24 or fc >= 40:
                        continue
                    for k in range(8):
                        mm(psmod[:, fc, :], wa[:, k, j * 128:(j + 1) * 128], scT[:, k, :], k == 0, k == 7,
                           [("wst", sl), "scT"], ["psmod"])
                gsel = None
                if 8 <= n < 12: gsel = (g1bc, (n - 8) * 256)
                if 20 <= n < 24: gsel = (g2bc, (n - 20) * 256)
                if gsel is not None:
                    for b in range(NB):
                        for k in range(8):
                            mm(psg[b][:, 0:256], scb[:, k, b, :], wa[:, k, :], k == 0, k == 7,
                               [("wst", sl)] + [("scb", k, b)], [("psg", b)])
                        P.dma("sync", bst[b][:, 0:256], b_ada[0:1, n * 256:(n + 1) * 256].to_broadcast([128, 256]), (), [("bst", b)])
                        tt("dve", gsel[0][:, b, gsel[1]:gsel[1] + 256], psg[b][:, 0:256], bst[b][:, 0:256], ALU.add,
                           [("psg", b), ("bst", b)], [("gbc", id(gsel[0]), b)])
            win_order = list(range(4, 16)) + [0, 1, 2, 3] + list(range(16, 20))
            load_cast(lambda n: Win[:, :, win_order[n] * 128:(win_order[n] + 1) * 128],
                      lambda n: w_in_v[:, :, win_order[n] * 128:(win_order[n] + 1) * 128], 20, "Win", (8, 128), engs=("dve", "act"))
            P.add("dve", lambda h: h.memset(modT[:], 0.0), (), ["modT"])
            for col in range(3):
                for lo_, hi_ in ((0, 16), (24, 40)):
                    tt("dve", modT[:, lo_:hi_, col], psmod[:, lo_:hi_, col], badT[:, lo_:hi_], ALU.add, ["psmod", "badT"], ["modT"])
            ts("dve", sc1p[:], modT[:, 8:16, :], 1.0, None, ALU.add, None, ["modT"], ["sc1p"])
            ts("dve", sc2p[:], modT[:, 32:40, :], 1.0, None, ALU.add, None, ["modT"], ["sc2p"])
            if dbg:
                P.dma("pool", dbg_out["d_mod"][:, :], modT[:].rearrange("p a b -> p (a b)"), ["modT"], [("dbg", "mod")])
            P.barrier()
            P.flush()

        def layernorm_stats(xt, st6, mv, rstd, eps_t, rk, key):
            P.add("dve", lambda h: h.bn_stats(out=st6[:, 0, :], in_=xt[:, 0:512]), rk, [(key, "st0")])
            P.add("dve", lambda h: h.bn_stats(out=st6[:, 1, :], in_=xt[:, 512:1024]), rk, [(key, "st1")])
            P.add("dve", lambda h: h.bn_aggr(out=mv[:], in_=st6[:].rearrange("p a b -> p (a b)")),
                  [(key, "st0"), (key, "st1")], [(key, "mv")])
            act(rstd[:], mv[:, 1:2], AF.Sqrt, [(key, "mv"), "epsln"], [(key, "rstd")], bias=eps_t[:])
            recip(rstd[:], rstd[:], [(key, "rstd")], [(key, "rstd")])

        L = {}

        def lru_alloc(es):
            for nm, shp, dt_ in (("ub", [128, 516], F32), ("ucvb", [128, 512], BF16), ("ra", [128, 512], F32),
                                 ("hsc", [128, 512], F32), ("yst", [128, 512], BF16), ("carry", [128, 1], F32)):
                L[nm] = [A(es, "%s%d" % (nm, i), shp, dt_) for i in range(4)]
            for nm in ("hfb", "gg", "ucv", "av", "ib", "m2"):
                L[nm] = [[A(es, "%s%d_%d" % (nm, i, j), [128, 512], F32) for j in range(2)] for i in range(4)]
            L["pz"] = [PS(es, "pz%d" % i, [128, 512]) for i in range(8)]

        def lru_pass(b, d):
            ub, ucv, ucvb, ra, av, ib, m2, hsc, hfb, gg, yst, carry, pz = (L[k] for k in (
                "ub", "ucv", "ucvb", "ra", "av", "ib", "m2", "hsc", "hfb", "gg", "yst", "carry", "pz"))
            order = list(range(9)) if d == 0 else [0] + list(range(8, 0, -1))
            NS = len(order)
            CH = range(4)

            def seginfo(seg):
                n = 256 if seg == 0 else 512
                t0 = 0 if seg == 0 else CL + (seg - 1) * 512
                qlo, qhi = (0, CL) if seg == 0 else (CL, T)
                return n, t0, (seg - 1) * 512, max(qlo, t0 - 2), min(qhi, t0 + n + 1)

            def loads(si):
                seg = order[si]
                n, t0, s0, lo, hi = seginfo(seg)
                for c in CH:
                    kub = ("ub", c)
                    if lo > t0 - 2:
                        P.add("pool", lambda h, c=c: h.memset(ub[c][:, 0:2], 0.0), (), [kub])
                    if hi < t0 + n + 1:
                        P.add("pool", lambda h, c=c, n=n: h.memset(ub[c][:, n + 2:n + 3], 0.0), (), [kub])
                    P.dma("sync", ub[c][:, lo - (t0 - 2):hi - (t0 - 2)], uT_s[b, c, :, lo:hi],
                          [("uT_s", b, c, g_) for g_ in (seg - 1, seg, seg + 1)], [kub])

            def loads_hg(si):
                seg = order[si]
                n, t0, s0, lo, hi = seginfo(seg)
                for c in CH:
                    if d == 1 and seg > 0:
                        P.dma("sync", hfb[c][si % 2][:], hf_s[b, c, :, s0:s0 + 512], [("hf_s", b, c, seg)], [("hfb", c, si % 2)])
                        P.dma("sync", gg[c][si % 2][:], gT_s[b, c, :, s0:s0 + 512], [("gT_s", b, c, seg)], [("gg", c, si % 2)])

            def conv(si):
                n = seginfo(order[si])[0]
                z = si % 2
                for c in CH:
                    ts("dve", ucv[c][z][:, 0:n], ub[c][:, 0:n], cw[:, c, 0:1], cb[:, c:c + 1], ALU.mult, ALU.add,
                       [("ub", c), "cw", "cb"], [("ucv", c, z)])
                for j in range(1, 4):
                    for c in CH:
                        stt(ucv[c][z][:, 0:n], ub[c][:, j:j + n], cw[:, c, j:j + 1], ucv[c][z][:, 0:n], ALU.mult, ALU.add,
                            [("ub", c), "cw"], [("ucv", c, z)])

            def stage_b(si):
                n = seginfo(order[si])[0]
                z = si % 2
                for c in CH:
                    cp("act", ucvb[c][:, 0:n], ucv[c][z][:, 0:n], [("ucv", c, z)], [("ucvb", c)])
                for c in CH:
                    gr = (d * 2 + 0) * 4 + c; gi_ = (d * 2 + 1) * 4 + c
                    mm(pz[2 * c][:, 0:n], wg[:, gr, :], ucvb[c][:, 0:n], True, True, [("ucvb", c), "wg"], [("pz", 2 * c)])
                    mm(pz[2 * c + 1][:, 0:n], wg[:, gi_, :], ucvb[c][:, 0:n], True, True, [("ucvb", c), "wg"], [("pz", 2 * c + 1)])
                for c in CH:
                    gr = (d * 2 + 0) * 4 + c; gi_ = (d * 2 + 1) * 4 + c
                    act(ra[c][:, 0:n], pz[2 * c][:, 0:n], AF.Sigmoid, ["gb"], [("pz", 2 * c), ("ra", c)], bias=gb[:, gr:gr + 1])
                    act(ib[c][z][:, 0:n], pz[2 * c + 1][:, 0:n], AF.Sigmoid, ["gb"], [("pz", 2 * c + 1), ("ib", c, z)], bias=gb[:, gi_:gi_ + 1])
                for c in CH:
                    act(av[c][z][:, 0:n], ra[c][:, 0:n], AF.Exp, [("ra", c), "c1"], [("av", c, z)], scale=c1[:, d * 4 + c:d * 4 + c + 1])
                    act(m2[c][z][:, 0:n], ra[c][:, 0:n], AF.Exp, [("ra", c), "c1x2"], [("m2", c, z)], scale=c1x2[:, d * 4 + c:d * 4 + c + 1])
                for c in CH:
                    tt("pool", ib[c][z][:, 0:n], ib[c][z][:, 0:n], ucv[c][z][:, 0:n], ALU.mult, [("ucv", c, z), ("ib", c, z)], [("ib", c, z)])
                for c in CH:
                    act(m2[c][z][:, 0:n], m2[c][z][:, 0:n], AF.Sqrt, [("m2", c, z), "one_t"], [("m2", c, z)], bias=one_t[:, 0:1], scale=-1.0)

            def stage_c(si):
                seg = order[si]
                n, t0, s0, lo, hi = seginfo(seg)
                z = si % 2
                for c in CH:
                    tt("pool", ib[c][z][:, 0:n], ib[c][z][:, 0:n], m2[c][z][:, 0:n], ALU.mult, [("m2", c, z), ("ib", c, z)], [("ib", c, z)])
                for c in CH:
                    o_ap = hsc[c][:, 0:n]; a_ap = av[c][z][:, 0:n]; b_ap = ib[c][z][:, 0:n]
                    if d == 1:
                        o_ap = o_ap[:, ::-1]; a_ap = a_ap[:, ::-1]; b_ap = b_ap[:, ::-1]
                    init = 0.0 if si == 0 else carry[c][:, 0:1]
                    P.add("dve", lambda h, o_ap=o_ap, a_ap=a_ap, b_ap=b_ap, init=init: h.tensor_tensor_scan(
                        out=o_ap, data0=a_ap, data1=b_ap, initial=init, op0=ALU.mult, op1=ALU.add),
                        [("av", c, z), ("ib", c, z), ("carry", c)], [("hsc", c)])
                for c in CH:
                    last_col = hsc[c][:, n - 1:n] if d == 0 else hsc[c][:, 0:1]
                    cp("dve", carry[c][:, 0:1], last_col, [("hsc", c)], [("carry", c)])
                if seg == 0:
                    return
                if d == 0:
                    for c in CH:
                        P.dma_defer("sync", hf_s[b, c, :, s0:s0 + 512], hsc[c][:, :], [("hsc", c)], [("hf_s", b, c, seg)])
                else:
                    G_ = [gg[c][z] for c in CH]; H_ = [hfb[c][z] for c in CH]
                    W_ = [m2[c][z] for c in CH]; S_ = [av[c][z] for c in CH]
                    for c in CH:
                        act(W_[c][:], G_[c][:], AF.Square, [("gg", c, z), ("m2", c, z)], [("m2", c, z)], scale=0.21145921128196543)
                    for c in CH:
                        stt(W_[c][:], W_[c][:], 1.0, G_[c][:], ALU.add, ALU.mult, [("m2", c, z), ("gg", c, z)], [("m2", c, z)])
                    for c in CH:
                        act(S_[c][:], W_[c][:], AF.Sigmoid, [("m2", c, z), ("av", c, z)], [("av", c, z)], scale=1.5957691216057308)
                    for c in CH:
                        tt("pool", H_[c][:], H_[c][:], hsc[c][:], ALU.add, [("hsc", c), ("hfb", c, z)], [("hfb", c, z)])
                    for c in CH:
                        tt("pool", S_[c][:], S_[c][:], G_[c][:], ALU.mult, [("gg", c, z), ("av", c, z)], [("av", c, z)])
                    for c in CH:
                        tt("dve", yst[c][:], H_[c][:], S_[c][:], ALU.mult, [("hfb", c, z), ("av", c, z)], [("yst", c)])
                    for c in CH:
                        P.dma_defer("sync", yT_s[b, :, c, s0:s0 + 512], yst[c][:], [("yst", c)], [("yT_s", b, c, seg)])

            loads(0)
            loads_hg(0)
            conv(0)
            if NS > 1:
                loads(1)
            for si in range(NS):
                if si + 1 < NS:
                    conv(si + 1)
                    if si + 2 < NS:
                        loads(si + 2)
                    loads_hg(si + 1)
                P.undefer()
                stage_b(si)
                stage_c(si)
            P.undefer()

        for b in range(NB):
            if stage < 1:
                break
            with ExitStack() as p24:
                with ExitStack() as p12:
                    KT = A(p12, "KT", [128, 4, T], BF16)
                    V = A(p12, "V", [128, 34, 4, 129], BF16)
                    with ExitStack() as p1:
                        xin = [A(p1, "xin%d" % i, [128, 1024]) for i in range(2)]
                        st6 = [A(p1, "st6%d" % i, [128, 2, 6]) for i in range(2)]
                        mv = [A(p1, "mv%d" % i, [128, 2]) for i in range(2)]
                        rstd = [A(p1, "rstd%d" % i, [128, 1]) for i in range(2)]
                        xn = [A(p1, "xn%d" % i, [128, 1024], BF16) for i in range(4)]
                        if os.environ.get("K_SWAP"):
                            hT = [A(p1, "hT%d" % i, [128, 8, 512], BF16) for i in range(2)][::-1]
                        else:
                            hT = [A(p1, "hT%d" % i, [128, 8, 512], BF16) for i in range(2)]
                        qkbf = [A(p1, "qkbf%d" % i, [128, 512], BF16) for i in range(2)]
                        qf = [A(p1, "qf%d" % i, [128, 512]) for i in range(2)]
                        t1 = [A(p1, "t1%d" % i, [128, 512]) for i in range(2)]
                        t2 = [A(p1, "t2%d" % i, [128, 512]) for i in range(2)]
                        cst = [A(p1, "cst%d" % i, [128, 512]) for i in range(2)]
                        snt = [A(p1, "snt%d" % i, [128, 512]) for i in range(2)]
                        ugst = [A(p1, "ugst%d" % i, [128, 512]) for i in range(4)]
                        qst = [A(p1, "qst%d" % i, [128, 4, 512], BF16) for i in range(2)]
                        pst = [PS(p1, "pst%d" % i, [128, 8, 128], BF16) for i in range(2)]
                        pm = [PS(p1, "pm%d" % i, [128, 512]) for i in range(4)]
                        prot = [PS(p1, "prot%d" % i, [128, 512]) for i in range(2)]

                        win_keys = []
                        P.add("pool", lambda h: h.memset(V[:, :, :, 128:129], 1.0), (), [("Vone", b)])
                        cnt1 = {"t": 0, "pm": 0, "rc": 0, "ug": 0}

                        def ginfo(gi):
                            ntile = 2 if gi == 0 else 4
                            return (ntile, ntile * 128, gi % 2, (2 if gi == 0 else b),
                                    (0 if gi == 0 else CL + (gi - 1) * 512), (gi - 1) * 512)

                        def lnA(gi, tiles=None):
                            ntile, ntok, hs, col, tok0, s0 = ginfo(gi)
                            for t in (range(ntile) if tiles is None else tiles):
                                xs = cnt1["t"] % 2; cnt1["t"] += 1
                                kx = ("x", b, xs)
                                src = ctx[b, t * 128:(t + 1) * 128, :] if gi == 0 else x[b, s0 + t * 128:s0 + (t + 1) * 128, :]
                                P.dma("sync", xin[xs][:], src, (), [kx])
                                if t == ntile - 1:
                                    if gi > 0:
                                        P.dma("sync", cst[gi % 2][:], cosT[:, s0:s0 + 512], (), [("cst", gi % 2)])
                                        P.dma("sync", snt[gi % 2][:], sinT[:, s0:s0 + 512], (), [("snt", gi % 2)])
                                    P.undefer()
                                layernorm_stats(xin[xs], st6[xs], mv[xs], rstd[xs], epsln, [kx], ("ln", xs))
                                ts("dve", xn[t][:], xin[xs][:], mv[xs][:, 0:1], rstd[xs][:, 0:1], ALU.subtract, ALU.mult,
                                   [kx, (("ln", xs), "mv"), (("ln", xs), "rstd")], [("xn", t)])

                        def lnB(gi):
                            ntile, ntok, hs, col, tok0, s0 = ginfo(gi)
                            for t in range(ntile):
                                xs = t % 2
                                for k in range(8):
                                    tr(pst[xs][:, k, :], xn[t][:, k * 128:(k + 1) * 128], [("xn", t), "ident"], [("pst", xs)])
                                for k in range(8):
                                    if xs == 0:
                                        act(hT[hs][:, k, t * 128:(t + 1) * 128], pst[xs][:, k, :], AF.Identity,
                                            ["sc1p", "modT"], [("pst", xs), ("hT", hs, t)],
                                            bias=modT[:, k, col:col + 1], scale=sc1p[:, k, col:col + 1], safe=True)
                                    else:
                                        ts("dve", hT[hs][:, k, t * 128:(t + 1) * 128], pst[xs][:, k, :], sc1p[:, k, col:col + 1],
                                           modT[:, k, col:col + 1], ALU.mult, ALU.add,
                                           ["sc1p", "modT"], [("pst", xs), ("hT", hs, t)], safe=True)

                        def proj_mm(gi, fc):
                            ntile, ntok, hs, col, tok0, s0 = ginfo(gi)
                            hkeys = [("hT", hs, t) for t in range(ntile)]
                            pb = cnt1["pm"] % 4; cnt1["pm"] += 1
                            for k in range(8):
                                mm(pm[pb][:, 0:ntok], Win[:, k, fc * 128:(fc + 1) * 128], hT[hs][:, k, 0:ntok], k == 0, k == 7,
                                   hkeys + win_keys, [("pm", pb)])
                            return pb

                        def rope_tail(gi, fc, rs):
                            ntile, ntok, hs, col, tok0, s0 = ginfo(gi)
                            qs_ = gi % 2; cs_ = gi % 2
                            mm(prot[rs][:, :], RTm[:], qkbf[rs][:], True, True, [("qkbf", rs), "RTm"], [("prot", rs)])
                            tt("pool", t1[rs][:], qf[rs][:], cst[cs_][:], ALU.mult, [("qf", rs), ("cst", cs_)], [("t1", rs)])
                            tt("dve", t2[rs][:], prot[rs][:, :], snt[cs_][:], ALU.mult, [("snt", cs_)], [("prot", rs), ("t2", rs)])
                            if fc < 4:
                                tt("pool", qst[qs_][:, fc, :], t1[rs][:], t2[rs][:], ALU.add, [("t1", rs), ("t2", rs)], [("qst", qs_, fc)])
                            else:
                                tt("pool", KT[:, fc - 4, tok0:tok0 + 512], t1[rs][:], t2[rs][:], ALU.add,
                                   [("t1", rs), ("t2", rs)], [("KT", b, fc - 4, gi)])

                        def proj_qk(gi):
                            ntile, ntok, hs, col, tok0, s0 = ginfo(gi)
                            pend = None
                            chunks = [4, 5, 6, 7] if gi == 0 else list(range(8))
                            for idx, fc in enumerate(chunks):
                                if gi + 1 < 9:
                                    if gi == 0:
                                        lnA(gi + 1, [idx])
                                    elif idx % 2 == 0:
                                        lnA(gi + 1, [idx // 2])
                                pb = proj_mm(gi, fc)
                                if gi == 0:
                                    cp("act", KT[:, fc - 4, 0:ntok], pm[pb][:, 0:ntok], [], [("pm", pb), ("KT", b, fc - 4, gi)])
                                else:
                                    rs = cnt1["rc"] % 2; cnt1["rc"] += 1
                                    cp("act", qf[rs][:], pm[pb][:, :], [], [("pm", pb), ("qf", rs)])
                                    cp("dve", qkbf[rs][:], qf[rs][:], [("qf", rs)], [("qkbf", rs)])
                                    if pend is not None:
                                        rope_tail(gi, *pend)
                                    pend = (fc, rs)
                            if pend is not None:
                                rope_tail(gi, *pend)
                            if gi > 0:
                                P.dma_defer("sync", qT_s[b, :, :, s0:s0 + 512], qst[gi % 2][:], [("qst", gi % 2, f) for f in range(4)],
                                            [("qT_s", b, gi - 1)])

                        def proj_vug(gi):
                            ntile, ntok, hs, col, tok0, s0 = ginfo(gi)
                            hkeys = [("hT", hs, t) for t in range(ntile)]
                            for t in range(ntile):
                                pb = cnt1["pm"] % 4; cnt1["pm"] += 1
                                for k in range(8):
                                    mm(pm[pb][:, :], hT[hs][:, k, t * 128:(t + 1) * 128], Win[:, k, 1024:1536], k == 0, k == 7,
                                       hkeys + win_keys, [("pm", pb)])
                                vt = tok0 // 128 + t
                                cp("dve" if t % 2 else "act", V[:, vt, :, 0:128], pm[pb][:, :].rearrange("p (h d) -> p h d", h=4),
                                   [], [("pm", pb), ("V", b, vt)])
                            for fc in range(12, 20):
                                if gi == 0 and fc >= 16:
                                    continue
                                pb = proj_mm(gi, fc)
                                us = cnt1["ug"] % 4; cnt1["ug"] += 1
                                cp("act" if fc % 2 else "dve", ugst[us][:, 0:ntok], pm[pb][:, 0:ntok], [], [("pm", pb), ("ugst", us)])
                                if fc < 16:
                                    P.dma_defer("sync", uT_s[b, fc - 12, :, tok0:tok0 + ntok], ugst[us][:, 0:ntok], [("ugst", us)],
                                                [("uT_s", b, fc - 12, gi)])
                                else:
                                    P.dma_defer("sync", gT_s[b, fc - 16, :, s0:s0 + 512], ugst[us][:, :], [("ugst", us)],
                                                [("gT_s", b, fc - 16, gi)])
                                if cnt1["ug"] % 4 == 0:
                                    P.undefer()

                        lnA(0)
                        lnB(0)
                        for gi in range(9):
                            proj_qk(gi)
                            if gi + 1 < 9:
                                lnB(gi + 1)
                            proj_vug(gi)
                        P.undefer()
                        if dbg and b == 0:
                            P.dma("pool", dbg_out["d_k"][:, :], KT[:].rearrange("p a b -> p (a b)"),
                                  [("KT", b, h_, gi_) for h_ in range(4) for gi_ in range(9)], [("dbg", "k")])
                            P.dma("pool", dbg_out["d_v"][:, :], V[:].rearrange("p a b c -> p (a b c)"),
                                  [("V", b, vt_) for vt_ in range(34)] + [("Vone", b)], [("dbg", "v")])
                        P.barrier()
                        P.flush()
                    if stage < 2:
                        continue
                    with ExitStack() as p2:
                        qblk = [A(p2, "qblk%d" % i, [128, 4, 512], BF16) for i in range(2)]
                        PT = [A(p2, "PT%d" % i, [128, 1024], BF16) for i in range(4)]
                        rl = A(p2, "rl", [128, 8]); nl = A(p2, "nl", [128, 4])
                        o1 = A(p2, "o1", [128, 4, 128]); od = A(p2, "od", [128, 4, 128]); sq = A(p2, "sq", [128, 4, 128])
                        ss = A(p2, "ss", [128, 4]); rr = A(p2, "rr", [128, 4])
                        onb = A(p2, "onb", [128, 4, 128], BF16)
                        ost = [A(p2, "ost%d" % i, [128, 512], BF16) for i in range(2)]
                        pss = [PS(p2, "pss%d" % i, [128, 1024]) for i in range(2)]
                        pso = [PS(p2, "pso%d" % i, [128, 512]) for i in range(3)]
                        ptr = PS(p2, "ptr", [128, 4, 128], BF16)
                        NKT = T // 128
                        ptc = 0; rnd = 0
                        NQB = int(os.environ.get("K_QB", "8"))
                        for QB in range(NQB):
                            qs_ = QB % 2
                            if QB == 0:
                                P.dma("sync", qblk[qs_][:], qT_s[b, :, :, QB * 512:(QB + 1) * 512], [("qT_s", b, QB)], [("qblk", qs_)])
                            for hh in range(4):
                                if hh == 1 and QB + 1 < NQB:
                                    P.dma("sync", qblk[1 - qs_][:], qT_s[b, :, :, (QB + 1) * 512:(QB + 2) * 512],
                                          [("qT_s", b, QB + 1)], [("qblk", 1 - qs_)])
                                def S_(kt):
                                    sb = kt % 2
                                    ksl = slice(kt * 128, (kt + 1) * 128)
                                    mm(pss[sb][:, 0:512], KT[0:64, hh, ksl], qblk[qs_][0:64, hh, :], True, True,
                                       [("qblk", qs_)], [("pss", sb)])
                                    mm(pss[sb][:, 512:1024], KT[64:128, hh, ksl], qblk[qs_][64:128, hh, :], True, True,
                                       [("qblk", qs_)], [("pss", sb)])
                                def PV_(kt, pt):
                                    for m in range(2):
                                        for qs in range(4):
                                            a = m * 4 + qs
                                            bank = a // 3; c0 = (a % 3) * 160
                                            mm(pso[bank][:, c0:c0 + 129], PT[pt][:, m * 512 + qs * 128:m * 512 + (qs + 1) * 128],
                                               V[:, kt, hh, :], (kt == 0 and a % 3 == 0), kt == NKT - 1,
                                               [("PT", pt)], [("pso", bank)], skip=True)
                                S_(0)
                                prev = None
                                for kt in range(NKT):
                                    if kt + 1 < NKT:
                                        S_(kt + 1)
                                    sb = kt % 2
                                    pt = ptc % 4; ptc += 1
                                    act(PT[pt][:], pss[sb][:, :], AF.Exp, [], [("pss", sb), ("PT", pt)], scale=0.125, safe=True)
                                    if prev is not None:
                                        PV_(*prev)
                                    prev = (kt, pt)
                                PV_(*prev)
                                def acc(a):
                                    return pso[a // 3][:, (a % 3) * 160:(a % 3) * 160 + 128]
                                for bank in range(3):
                                    n = 3 if bank < 2 else 2
                                    recip(rl[:, bank * 3:bank * 3 + n], pso[bank][:, 128:128 + 160 * (n - 1) + 1:160],
                                          [], [("pso", bank), ("rl", bank)])
                                ts("dve", nl[:], rl[:, 4:8], lam_neg[:, 0:1], None, ALU.mult, None,
                                   [("rl", 1), ("rl", 2), "lam_neg"], ["nl"])
                                for qs in range(4):
                                    ts("dve", o1[:, qs, :], acc(qs), rl[:, qs:qs + 1], None, ALU.mult, None,
                                       [("rl", 0), ("rl", 1)], [("pso", qs // 3), ("o1", qs)])
                                for qs in range(4):
                                    a = 4 + qs
                                    stt(od[:, qs, :], acc(a), nl[:, qs:qs + 1], o1[:, qs, :], ALU.mult, ALU.add,
                                        ["nl", ("o1", qs)], [("pso", a // 3), ("od", qs)])
                                tt("dve", sq[:], od[:], od[:], ALU.mult, [("od", q_) for q_ in range(4)], ["sq"])
                                P.add("dve", lambda h: h.tensor_reduce(out=ss[:], in_=sq[:], axis=AX.X, op=ALU.add), ["sq"], ["ss"])
                                act(rr[:], ss[:], AF.Ln, ["ss", "epsrms"], ["rr"], bias=epsrms[:], scale=1.0 / 128.0)
                                act(rr[:], rr[:], AF.Exp, ["rr"], ["rr"], scale=-0.5)
                                for qs in range(4):
                                    stt(onb[:, qs, :], od[:, qs, :], rr[:, qs:qs + 1], subg[:], ALU.mult, ALU.mult,
                                        ["rr", ("od", qs), "subg"], [("onb", qs)])
                                for qs in range(4):
                                    tr(ptr[:, qs, :], onb[:, qs, :], [("onb", qs), "ident"], ["ptr"])
                                os_ = rnd % 2; rnd += 1
                                cp("dve", ost[os_][:], ptr[:].rearrange("p a b -> p (a b)"), [], ["ptr", ("ost", os_)])
                                P.dma("sync", oT_s[b, :, hh, QB * 512:(QB + 1) * 512], ost[os_][:], [("ost", os_)],
                                      [("oT_s", b, hh, QB)])
                        if dbg and b == 0:
                            nq_ = int(os.environ.get("K_QB", "8")) * 512
                            P.dma("sync", dbg_out["d_o"].rearrange("p (a b) -> p a b", a=4)[:, :, 0:nq_], oT_s[b, :, :, 0:nq_],
                                  [("oT_s", b, h_, q_) for h_ in range(4) for q_ in range(8)], [("dbg", "o")])
                        P.barrier()
                        P.flush()
                if stage < 3:
                    continue
                with ExitStack() as p3:
                    lru_alloc(p3)
                    for d in range(2):
                        lru_pass(b, d)
                    if dbg and b == 0:
                        P.dma("sync", dbg_out["d_y"].rearrange("p (a b) -> p a b", a=4), yT_s[b, :, :, :],
                              [("yT_s", b, c_, s_) for c_ in range(4) for s_ in range(1, 9)], [("dbg", "y")])
                    P.barrier()
                    P.flush()
                if stage < 4:
                    continue
                with ExitStack() as p4:
                    Wout = A(p4, "Wout", [128, 8, 1024], BF16)
                    ln1g = A(p4, "ln1g", [128, 1024]); ln1b = A(p4, "ln1b", [128, 1024])
                    oyT = [A(p4, "oyT%d" % i, [128, 8, 512], BF16) for i in range(2)]
                    wst[0] = A(p4, "wst0", [128, WST]); wst[1] = A(p4, "wst1", [128, WST])
                    xin4 = [A(p4, "xin4%d" % i, [128, 1024]) for i in range(4)]
                    tt4 = [A(p4, "tt4%d" % i, [128, 1024]) for i in range(8)]
                    st64 = [A(p4, "st64%d" % i, [128, 2, 6]) for i in range(4)]
                    mv4 = [A(p4, "mv4%d" % i, [128, 2]) for i in range(4)]
                    rstd4 = [A(p4, "rstd4%d" % i, [128, 1]) for i in range(4)]
                    nmr4 = [A(p4, "nmr4%d" % i, [128, 1]) for i in range(4)]
                    py = [PS(p4, "py%d" % i, [128, 1024]) for i in range(4)]
                    load_cast(lambda n: Wout[:, :, n * 256:(n + 1) * 256], lambda n: w_out_v[:, :, n * 256:(n + 1) * 256],
                              4, ("Wout", b), (8, 256), engs=("dve", "act"))
                    wout_keys = [(("Wout", b), n) for n in range(4)]
                    P.dma("sync", ln1g[:], ln1[0:1, :].to_broadcast([128, 1024]), (), ["ln1g"])
                    P.dma("sync", ln1b[:], ln1[1:2, :].to_broadcast([128, 1024]), (), ["ln1b"])
                    for G in range(8):
                        gs = G % 2
                        P.dma("sync", oyT[gs][:, 0:4, :], oT_s[b, :, :, G * 512:(G + 1) * 512],
                              [("oT_s", b, h_, G) for h_ in range(4)], [("oyT", gs, 0)])
                        P.dma("sync", oyT[gs][:, 4:8, :], yT_s[b, :, :, G * 512:(G + 1) * 512],
                              [("yT_s", b, c_, G + 1) for c_ in range(4)], [("oyT", gs, 1)])
                        TT = range(4)
                        for t in TT:
                            tok = G * 512 + t * 128
                            P.dma("sync", xin4[t][:], x[b, tok:tok + 128, :], (), [("x4", t)])
                        P.undefer()
                        for t in TT:
                            for half in range(2):
                                for k in range(8):
                                    mm(py[t][:, half * 512:(half + 1) * 512], oyT[gs][:, k, t * 128:(t + 1) * 128],
                                       Wout[:, k, half * 512:(half + 1) * 512], k == 0, k == 7,
                                       [("oyT", gs, 0), ("oyT", gs, 1)] + wout_keys, [("py", t)])
                        for t in TT:
                            tt("dve", tt4[t + 4 * gs][:], py[t][:, :], g1bc[:, b, :], ALU.mult, [("gbc", id(g1bc), b)], [("py", t), ("tt4", t + 4 * gs)])
                        for t in TT:
                            stt(tt4[t + 4 * gs][:], xin4[t][:], ALPHA, tt4[t + 4 * gs][:], ALU.mult, ALU.add, [("x4", t), ("tt4", t + 4 * gs)], [("tt4", t + 4 * gs)])
                        for t in TT:
                            P.add("dve", lambda h, t=t, gs=gs: h.bn_stats(out=st64[t][:, 0, :], in_=tt4[t + 4 * gs][:, 0:512]), [("tt4", t + 4 * gs)], [("st4a", t)])
                            P.add("dve", lambda h, t=t, gs=gs: h.bn_stats(out=st64[t][:, 1, :], in_=tt4[t + 4 * gs][:, 512:1024]), [("tt4", t + 4 * gs)], [("st4b", t)])
                        for t in TT:
                            P.add("dve", lambda h, t=t: h.bn_aggr(out=mv4[t][:], in_=st64[t][:].rearrange("p a b -> p (a b)")),
                                  [("st4a", t), ("st4b", t)], [("mv4", t)])
                        for t in TT:
                            act(rstd4[t][:], mv4[t][:, 1:2], AF.Sqrt, [("mv4", t), "epsln"], [("rstd4", t)], bias=epsln[:])
                        for t in TT:
                            recip(rstd4[t][:], rstd4[t][:], [("rstd4", t)], [("rstd4", t)])
                        for t in TT:
                            stt(nmr4[t][:], mv4[t][:, 0:1], -1.0, rstd4[t][:, 0:1], ALU.mult, ALU.mult,
                                [("mv4", t), ("rstd4", t)], [("nmr4", t)])
                        for t in TT:
                            act(tt4[t + 4 * gs][:], tt4[t + 4 * gs][:], AF.Identity, [("tt4", t + 4 * gs), ("nmr4", t), ("rstd4", t)], [("tt4", t + 4 * gs)],
                                bias=nmr4[t][:, 0:1], scale=rstd4[t][:, 0:1])
                        for t in TT:
                            tt("pool", tt4[t + 4 * gs][:], tt4[t + 4 * gs][:], ln1g[:], ALU.mult, [("tt4", t + 4 * gs), "ln1g"], [("tt4", t + 4 * gs)])
                        for t in TT:
                            tt("pool", tt4[t + 4 * gs][:], tt4[t + 4 * gs][:], ln1b[:], ALU.add, [("tt4", t + 4 * gs), "ln1b"], [("tt4", t + 4 * gs)])
                        for t in TT:
                            tok = G * 512 + t * 128
                            P.dma_defer("sync", x1_s[b * S + tok:b * S + tok + 128, :], tt4[t + 4 * gs][:], [("tt4", t + 4 * gs)], [("x1_s", b, G, t)])
                    P.undefer()
                    if dbg and b == 0:
                        P.dma("sync", dbg_out["d_x1"][:, :], x1_s[0:S, :], [("x1_s", b, G_, t_) for G_ in range(8) for t_ in range(4)],
                              [("dbg", "x1")])
                    P.barrier()
                    P.flush()
        P.barrier()
        P.flush()
        gm.close()
        if stage >= 5:
            with ExitStack() as p5:
                W1 = A(p5, "W1", [128, 8, 2 * FH], BF16)
                W2 = A(p5, "W2", [128, 22, 1024], BF16)
                ln2g = A(p5, "ln2g", [128, 1024]); ln2b = A(p5, "ln2b", [128, 1024])
                xin5 = [A(p5, "xin5%d" % i, [128, 1024]) for i in range(4)]
                tt5 = [A(p5, "tt5%d" % i, [128, 1024]) for i in range(2)]
                xn5 = [A(p5, "xn5%d" % i, [128, 1024], BF16) for i in range(2)]
                h2T = [A(p5, "h2T%d" % i, [128, 8, 256], BF16) for i in range(2)]
                sa = [A(p5, "sa%d" % i, [128, 256]) for i in range(4)]
                actT = A(p5, "actT", [128, 22, 256], BF16)
                st65 = [A(p5, "st65%d" % i, [128, 2, 6]) for i in range(2)]
                mv5 = [A(p5, "mv5%d" % i, [128, 2]) for i in range(2)]
                rstd5 = [A(p5, "rstd5%d" % i, [128, 1]) for i in range(2)]
                st65o = [A(p5, "st65o%d" % i, [128, 2, 6]) for i in range(2)]
                mv5o = [A(p5, "mv5o%d" % i, [128, 2]) for i in range(2)]
                rstd5o = [A(p5, "rstd5o%d" % i, [128, 1]) for i in range(2)]
                pst5 = [PS(p5, "pst5%d" % i, [128, 8, 128], BF16) for i in range(2)]
                pAB = [PS(p5, "pAB%d" % i, [128, 512]) for i in range(4)]
                pf = PS(p5, "pf", [128, 1024])
                stg5 = tt5 + xin5
                stg5k = [("tt5", 0), ("tt5", 1)] + [("x5", i) for i in range(4)]
                w1_order = [c_ for j_ in range(22) for c_ in (j_, 22 + j_)]
                load_cast(lambda n: W1[:, :, w1_order[n] * 128:(w1_order[n] + 1) * 128],
                          lambda n: w_f1_v[:, :, w1_order[n] * 128:(w1_order[n] + 1) * 128],
                          44, "W1o", (8, 128), stg=stg5, skey=stg5k, engs=("dve", "act"))
                load_cast(lambda n: W2[:, n:n + 1, :], lambda n: w_f2_v[:, n:n + 1, :], 22, "W2", (1, 1024), stg=stg5, skey=stg5k,
                          engs=("dve", "act"))
                w2_keys = []
                P.dma("sync", ln2g[:], ln2[0:1, :].to_broadcast([128, 1024]), (), ["ln2g"])
                P.dma("sync", ln2b[:], ln2[1:2, :].to_broadcast([128, 1024]), (), ["ln2b"])
                abc = [0]
                groups = [(b_, G_) for b_ in range(NB) for G_ in range(S // 256)]

                def lnt(gidx):
                    b_, G = groups[gidx]
                    hs = gidx % 2
                    for t in range(2):
                        xs = (gidx * 2 + t) % 4
                        ps_ = (gidx * 2 + t) % 2
                        tok = G * 256 + t * 128
                        kx = ("x5", xs)
                        P.dma("sync", xin5[xs][:], x1_s[b_ * S + tok:b_ * S + tok + 128, :],
                              [("x1_s", b_, tok // 512, (tok % 512) // 128)], [kx])
                        layernorm_stats(xin5[xs], st65[ps_], mv5[ps_], rstd5[ps_], epsln, [kx], ("ln5", ps_))
                        ts("dve", xn5[ps_][:], xin5[xs][:], mv5[ps_][:, 0:1], rstd5[ps_][:, 0:1], ALU.subtract, ALU.mult,
                           [kx, (("ln5", ps_), "mv"), (("ln5", ps_), "rstd")], [("xn5", ps_)])
                        for k in range(8):
                            tr(pst5[ps_][:, k, :], xn5[ps_][:, k * 128:(k + 1) * 128], [("xn5", ps_), "ident"], [("pst5", ps_)])
                        for k in range(8):
                            if ps_ == 0:
                                act(h2T[hs][:, k, t * 128:(t + 1) * 128], pst5[ps_][:, k, :], AF.Identity,
                                    ["sc2p", "modT"], [("pst5", ps_), ("h2T", hs, t)],
                                    bias=modT[:, 24 + k, b_:b_ + 1], scale=sc2p[:, k, b_:b_ + 1], safe=True)
                            else:
                                ts("dve", h2T[hs][:, k, t * 128:(t + 1) * 128], pst5[ps_][:, k, :], sc2p[:, k, b_:b_ + 1],
                                   modT[:, 24 + k, b_:b_ + 1], ALU.mult, ALU.add,
                                   ["sc2p", "modT"], [("pst5", ps_), ("h2T", hs, t)], safe=True)

                def ffn_in(gidx):
                    hs = gidx % 2
                    hk = [("h2T", hs, 0), ("h2T", hs, 1)]
                    for j in range(22):
                        ia = abc[0] % 4; abc[0] += 1
                        pa = pAB[ia][:, 0:256]; pb_ = pAB[ia][:, 256:512]
                        kb = ("pAB", ia)
                        for k in range(8):
                            mm(pa, W1[:, k, j * 128:(j + 1) * 128], h2T[hs][:, k, :], k == 0, k == 7, hk + [("W1o", 2 * j)], [kb])
                        for k in range(8):
                            mm(pb_, W1[:, k, FH + j * 128:FH + (j + 1) * 128], h2T[hs][:, k, :], k == 0, k == 7,
                               hk + [("W1o", 2 * j + 1)], [kb])
                        act(sa[ia][:], pa, AF.Sigmoid, [], [kb, ("sa", ia)])
                        tt("dve", sa[ia][:], sa[ia][:], pa, ALU.mult, [], [kb, ("sa", ia)])
                        tt("dve", actT[:, j, :], sa[ia][:], pb_, ALU.mult, [("sa", ia)], [kb, ("actT", j)])

                def ffn_out(gidx):
                    b_, G = groups[gidx]
                    akeys = [("actT", j) for j in range(22)]
                    for t in range(2):
                        xs = (gidx * 2 + t) % 4
                        ts_ = (gidx * 2 + t) % 2
                        tok = G * 256 + t * 128
                        for half in range(2):
                            for j in range(22):
                                mm(pf[:, half * 512:(half + 1) * 512], actT[:, j, t * 128:(t + 1) * 128],
                                   W2[:, j, half * 512:(half + 1) * 512], j == 0, j == 21, akeys + [("W2", j)], ["pf"])
                        kt5 = ("tt5", ts_); kx = ("x5", xs); kl = ("ln5o", ts_)
                        tt("dve", tt5[ts_][:], pf[:, :], g2bc[:, b_, :], ALU.mult, [("gbc", id(g2bc), b_)], ["pf", kt5])
                        stt(tt5[ts_][:], xin5[xs][:], ALPHA, tt5[ts_][:], ALU.mult, ALU.add, [kx, kt5], [kt5])
                        layernorm_stats(tt5[ts_], st65o[ts_], mv5o[ts_], rstd5o[ts_], epsln, [kt5], kl)
                        ts("dve", tt5[ts_][:], tt5[ts_][:], mv5o[ts_][:, 0:1], rstd5o[ts_][:, 0:1], ALU.subtract, ALU.mult,
                           [kt5, (kl, "mv"), (kl, "rstd")], [kt5])
                        tt("pool", tt5[ts_][:], tt5[ts_][:], ln2g[:], ALU.mult, [kt5, "ln2g"], [kt5])
                        tt("pool", tt5[ts_][:], tt5[ts_][:], ln2b[:], ALU.add, [kt5, "ln2b"], [kt5])
                        P.dma("sync", out[b_, tok:tok + 128, :], tt5[ts_][:], [kt5], [("out", b_, tok)])

                lnt(0)
                for gidx in range(len(groups)):
                    ffn_in(gidx)
                    if gidx + 1 < len(groups):
                        lnt(gidx + 1)
                    ffn_out(gidx)
                P.barrier()
                P.flush()
        P.barrier()
        P.flush()
    return nc


def _prep_inputs(inp):
    f32 = np.float32
    x = np.ascontiguousarray(inp["x"], f32); c = np.asarray(inp["c"], f32); ctx = np.ascontiguousarray(inp["ctx"], f32)
    c_ctx = np.asarray(inp["c_ctx"], f32)
    common = {}
    common["w_ada"] = np.ascontiguousarray(inp["w_ada"][0], f32)
    ba = np.asarray(inp["b_ada"][0], f32)
    common["b_adaT"] = np.ascontiguousarray(ba.reshape(48, 128).T)
    common["b_ada"] = np.ascontiguousarray(ba.reshape(1, 6144))
    common["w_in"] = np.ascontiguousarray(inp["w_in"][0], f32)
    common["lamqk"] = np.ascontiguousarray(np.stack([inp["lam_q"][0].reshape(128), inp["lam_k"][0].reshape(128)]).astype(f32))
    common["subg"] = np.ascontiguousarray(inp["subln_g"][0].reshape(1, 128), f32)
    cwt = np.asarray(inp["conv_w"][0], f32)
    common["conv_wT"] = np.ascontiguousarray(cwt.reshape(4, 4, 128).transpose(2, 1, 0))
    common["conv_bT"] = np.ascontiguousarray(np.asarray(inp["conv_b"][0], f32).reshape(4, 128).T)
    gbv = np.asarray(inp["lru_b_gates"][0], f32).reshape(2, 2, 4, 128)
    common["gate_bT"] = np.ascontiguousarray(gbv.transpose(3, 0, 1, 2).reshape(128, 16))
    ll = np.asarray(inp["lru_lambda"][0], f32).reshape(2, 4, 128)
    common["lru_lamT"] = np.ascontiguousarray(ll.transpose(2, 0, 1).reshape(128, 8))
    wgt = np.asarray(inp["lru_w_gates"][0], f32)
    bd = np.zeros((2, 2, 4, 128, 128), f32)
    for cc in range(4):
        bd[:, :, cc, 0:64, 0:64] = wgt[:, :, 2 * cc]
        bd[:, :, cc, 64:128, 64:128] = wgt[:, :, 2 * cc + 1]
    common["wg_bd"] = np.ascontiguousarray(bd.reshape(16, 128, 128))
    common["w_out"] = np.ascontiguousarray(inp["w_out"][0], f32)
    common["ln1"] = np.ascontiguousarray(np.stack([inp["ln1_g"][0], inp["ln1_b"][0]]).astype(f32))
    common["w_f1"] = np.ascontiguousarray(inp["w_ffn_in"][0], f32)
    common["w_f2"] = np.ascontiguousarray(inp["w_ffn_out"][0], f32)
    common["ln2"] = np.ascontiguousarray(np.stack([inp["ln2_g"][0], inp["ln2_b"][0]]).astype(f32))
    common["ident"] = np.eye(128, dtype=f32).astype(ml_dtypes.bfloat16)
    RT = np.zeros((128, 128), f32)
    for fo in range(128):
        if fo % 32 < 16:
            RT[fo + 16, fo] = -1.0
        else:
            RT[fo - 16, fo] = 1.0
    common["RT"] = RT.astype(ml_dtypes.bfloat16)
    inv_freq = (np.float32(10000.0) ** (-np.arange(16, dtype=f32) / np.float32(16))).astype(f32)
    tok = np.arange(S)
    row = (tok // 64).astype(f32); colp = (tok % 64).astype(f32)
    ang_r = (row[None, :] * inv_freq[:, None]).astype(f32)
    ang_c = (colp[None, :] * inv_freq[:, None]).astype(f32)
    a64 = np.concatenate([ang_r, ang_r, ang_c, ang_c], axis=0)
    a128 = np.concatenate([a64, a64], axis=0)
    common["cosT"] = np.ascontiguousarray(np.cos(a128).astype(f32))
    common["sinT"] = np.ascontiguousarray(np.sin(a128).astype(f32))
    in_maps = []
    for i in range(NCORES):
        m = dict(common)
        m["x"] = x[NB * i:NB * (i + 1)]
        m["ctx"] = ctx[NB * i:NB * (i + 1)]
        cv = np.stack([c[NB * i], c[NB * i + 1], c_ctx], axis=1)
        m["cT"] = np.ascontiguousarray(cv.reshape(8, 128, 3).transpose(1, 0, 2))
        in_maps.append(m)
    return in_maps


def kernel(**inputs):
    in_maps = _prep_inputs(inputs)
    nc = build()
    res = run_bass_kernel_spmd(nc, in_maps, core_ids=list(range(NCORES)))
    outs = [np.asarray(r["out"], np.float32) for r in res.results]
    return np.concatenate(outs, axis=0)
```

```python
import os
import numpy as np
from contextlib import ExitStack
import concourse.bass as bass
import concourse.mybir as mybir
from concourse.bass_utils import run_bass_kernel_spmd
import ml_dtypes

F32 = mybir.dt.float32
BF16 = mybir.dt.bfloat16
AF = mybir.ActivationFunctionType
ALU = mybir.AluOpType
AX = mybir.AxisListType

D = 1024
S = 4096
CL = 256
T = S + CL
NB = 2
NCORES = 8
FH = 2816
ALPHA = float(2.0 ** 0.25)
LN_EPS = 1e-6
RMS_EPS = 1e-5
WST = 2048
ROPE_ENG = os.environ.get("K_ROPE", "pool")


class Op:
    __slots__ = ("eng", "fn", "r", "w", "dma", "safe", "waits", "marked", "tok", "snap", "bar")

    def __init__(self, eng, fn, r, w, dma, safe, bar=False):
        self.eng = eng; self.fn = fn; self.r = r; self.w = w; self.dma = dma; self.safe = safe
        self.waits = (); self.marked = False; self.tok = None; self.snap = None; self.bar = bar


class Prog:
    ENG = ("pe", "act", "dve", "pool", "sync")
    KD = 12

    def __init__(self, nc, es):
        self.nc = nc
        self.h = {"pe": nc.tensor, "act": nc.scalar, "dve": nc.vector, "pool": nc.gpsimd, "sync": nc.sync}
        self.sem = {e: es.enter_context(nc.semaphore("s_" + e)) for e in self.ENG}
        self.cnt = {e: 0 for e in self.ENG}
        self.dsem = {e: [es.enter_context(nc.semaphore("d_%s_%d" % (e, i))) for i in range(self.KD)]
                     for e in ("sync", "pool")}
        self.dcnt = {e: 0 for e in ("sync", "pool")}
        self.ops = []
        self.deferred = []
        self.start = 0
        self.last_w = {}
        self.readers = {}
        self.known = {e: {} for e in self.ENG}
        self.known_dma = {e: set() for e in self.ENG}
        self.last_real = {e: None for e in self.ENG}
        self.dma_out = []

    def add(self, eng, fn, r=(), w=(), dma=False, safe=False):
        self.ops.append(Op(eng, fn, tuple(r), tuple(w), dma, safe))

    def dma(self, eng, out, in_, r=(), w=()):
        if eng == "pool":
            eng = "sync"
        self.add(eng, lambda h: h.dma_start(out=out, in_=in_), r, w, dma=True)

    def dma_defer(self, eng, out, in_, r=(), w=()):
        if eng == "pool":
            eng = "sync"
        self.deferred.append(Op(eng, lambda h: h.dma_start(out=out, in_=in_), tuple(r), tuple(w), True, False))

    def undefer(self):
        self.ops.extend(self.deferred)
        self.deferred = []

    def barrier(self):
        self.undefer()
        for e in self.ENG:
            self.ops.append(Op(e, None, (), (), False, False, bar=True))

    def flush(self):
        ops = self.ops
        for i in range(self.start, len(ops)):
            op = ops[i]
            E = op.eng
            deps = set()
            if op.bar:
                for F in self.ENG:
                    if self.last_real[F] is not None:
                        deps.add(self.last_real[F])
                deps.update(self.dma_out)
            for k in op.r:
                j = self.last_w.get(k)
                if j is not None:
                    deps.add(j)
            for k in op.w:
                j = self.last_w.get(k)
                if j is not None:
                    deps.add(j)
                deps.update(self.readers.get(k, ()))
            deps.discard(i)
            wc = {}
            wd = []
            kn = self.known[E]
            for j in deps:
                pj = ops[j]
                if pj.dma:
                    if j not in self.known_dma[E]:
                        wd.append(j)
                else:
                    F = pj.eng
                    if F == E and (E == "pe" or op.safe) and not op.bar:
                        continue
                    if kn.get(F, -1) >= j:
                        continue
                    if wc.get(F, -1) < j:
                        wc[F] = j
            for F, j in wc.items():
                ops[j].marked = True
                if kn.get(F, -1) < j:
                    kn[F] = j
            for j in list(wc.values()) + wd:
                sn = ops[j].snap
                if sn:
                    for G, v in sn.items():
                        if kn.get(G, -1) < v:
                            kn[G] = v
            for j in wd:
                self.known_dma[E].add(j)
            op.waits = list(wc.values()) + wd
            op.snap = dict(kn)
            if op.fn is not None:
                for k in op.r:
                    self.readers.setdefault(k, []).append(i)
                for k in op.w:
                    self.last_w[k] = i
                    self.readers[k] = []
                if op.dma:
                    self.dma_out.append(i)
                else:
                    self.last_real[E] = i
        for i in range(self.start, len(ops)):
            op = ops[i]
            E = op.eng
            h = self.h[E]
            for j in op.waits:
                pj = ops[j]
                assert pj.tok is not None, (i, j, op.eng, pj.eng, op.r, op.w, pj.r, pj.w)
                if pj.dma:
                    h.wait_ge(pj.tok[0], pj.tok[1])
                else:
                    h.wait_ge(self.sem[pj.eng], pj.tok)
            if op.fn is None:
                continue
            if op.dma:
                n = self.dcnt[E]
                sem = self.dsem[E][n % self.KD]
                val = 16 * (n // self.KD + 1)
                if n >= self.KD:
                    h.wait_ge(sem, val - 16)
                ins = op.fn(h)
                ins.then_inc(sem, 16)
                op.tok = (sem, val)
                self.dcnt[E] = n + 1
            else:
                ins = op.fn(h)
                if op.marked:
                    self.cnt[E] += 1
                    ins.then_inc(self.sem[E], 1)
                    op.tok = self.cnt[E]
            op.fn = None
        if any(o.bar for o in ops[self.start:]):
            for E in self.ENG:
                for F in self.ENG:
                    lr = self.last_real[F]
                    if lr is not None:
                        if self.known[E].get(F, -1) < lr:
                            self.known[E][F] = lr
                self.known_dma[E] = set()
            self.dma_out = []
        self.start = len(ops)


def build(stage=99, dbg=False):
    nc = bass.Bass("TRN2", target_bir_lowering=False)

    def din(name, shape, dt=F32):
        return nc.dram_tensor(name, shape, dt, kind="ExternalInput").ap()

    x = din("x", [NB, S, D]); ctx = din("ctx", [NB, CL, D]); cT = din("cT", [128, 8, 3])
    w_ada = din("w_ada", [D, 6144]); b_adaT = din("b_adaT", [128, 48]); b_ada = din("b_ada", [1, 6144])
    w_in = din("w_in", [D, 2560]); lamqk = din("lamqk", [2, 128]); subg_d = din("subg", [1, 128])
    conv_wT = din("conv_wT", [128, 4, 4]); conv_bT = din("conv_bT", [128, 4]); gate_bT = din("gate_bT", [128, 16])
    lru_lamT = din("lru_lamT", [128, 8]); wg_bd = din("wg_bd", [16, 128, 128])
    w_out = din("w_out", [D, D]); ln1 = din("ln1", [2, D]); w_f1 = din("w_f1", [D, 2 * FH])
    w_f2 = din("w_f2", [FH, D]); ln2 = din("ln2", [2, D])
    ident_d = din("ident", [128, 128], BF16); RT_d = din("RT", [128, 128], BF16)
    cosT = din("cosT", [128, S]); sinT = din("sinT", [128, S])
    out = nc.dram_tensor("out", [NB, S, D], F32, kind="ExternalOutput").ap()
    dbg_out = {}
    if dbg:
        for nm, shp, dt_ in (("d_mod", [128, 144], F32), ("d_k", [128, 4 * T], BF16), ("d_v", [128, 34 * 4 * 129], BF16), ("d_o", [128, 4 * S], BF16),
                             ("d_y", [128, 4 * S], BF16), ("d_x1", [S, D], F32)):
            dbg_out[nm] = nc.dram_tensor(nm, shp, dt_, kind="ExternalOutput").ap()

    uT_s = nc.dram_tensor("uT_s", [NB, 4, 128, T], F32).ap()
    gT_s = nc.dram_tensor("gT_s", [NB, 4, 128, S], F32).ap()
    hf_s = nc.dram_tensor("hf_s", [NB, 4, 128, S], F32).ap()
    qT_s = nc.dram_tensor("qT_s", [NB, 128, 4, S], BF16).ap()
    x1_s = nc.dram_tensor("x1_s", [NB * S, D], F32).ap()
    oT_s = nc.dram_tensor("oT_s", [NB, 128, 4, S], BF16).ap()
    yT_s = nc.dram_tensor("yT_s", [NB, 128, 4, S], BF16).ap()

    w_ada_v = w_ada.rearrange("(k p) n -> p k n", p=128)
    w_in_v = w_in.rearrange("(k p) n -> p k n", p=128)
    w_out_v = w_out.rearrange("(k p) n -> p k n", p=128)
    w_f1_v = w_f1.rearrange("(k p) n -> p k n", p=128)
    w_f2_v = w_f2.rearrange("(j p) n -> p j n", p=128)

    with ExitStack() as g:
        P = Prog(nc, g)

        uid = [0]

        SB_TOP = 229344
        SB_LINE = 196608

        def A(es, name, shape, dt=F32):
            uid[0] += 1
            nbytes = int(np.prod(shape[1:])) * (2 if dt == BF16 else 4)
            off = SB_TOP - nc.sbuf_bytes_remaining
            if False and off < SB_LINE < off + nbytes + 64:
                padb = SB_LINE - off
                es.enter_context(nc.sbuf_tensor("pad%d" % uid[0], [128, (padb + 3) // 4], F32))
            return es.enter_context(nc.sbuf_tensor("sb%d_%s" % (uid[0], name), shape, dt))

        def PS(es, name, shape, dt=F32):
            uid[0] += 1
            return es.enter_context(nc.psum_tensor("ps%d_%s" % (uid[0], name), shape, dt))

        def mm(out_, lhsT, rhs, start, stop, r, w, skip=False):
            if skip:
                P.add("pe", lambda h: h.matmul(out_, lhsT=lhsT, rhs=rhs, start=start, stop=stop, skip_group_check=True), r, w)
            else:
                P.add("pe", lambda h: h.matmul(out_, lhsT=lhsT, rhs=rhs, start=start, stop=stop), r, w)

        def tr(out_, in_, r, w):
            P.add("pe", lambda h: h.transpose(out=out_, in_=in_, identity=ident[:]), r, w)

        def act(out_, in_, func, r, w, bias=None, scale=None, accum=None, safe=False):
            kw = {}
            if bias is not None: kw["bias"] = bias
            if scale is not None: kw["scale"] = scale
            if accum is not None: kw["accum_out"] = accum
            P.add("act", lambda h: h.activation(out=out_, in_=in_, func=func, **kw), r, w, safe=safe)

        def ts(eng, out_, in0, s1, s2, op0, op1, r, w, safe=False):
            if s2 is None:
                P.add(eng, lambda h: h.tensor_scalar(out=out_, in0=in0, scalar1=s1, scalar2=None, op0=op0), r, w, safe=safe)
            else:
                P.add(eng, lambda h: h.tensor_scalar(out=out_, in0=in0, scalar1=s1, scalar2=s2, op0=op0, op1=op1), r, w, safe=safe)

        def tt(eng, out_, in0, in1, op, r, w, safe=False):
            P.add(eng, lambda h: h.tensor_tensor(out=out_, in0=in0, in1=in1, op=op), r, w, safe=safe)

        def stt(out_, in0, sc, in1, op0, op1, r, w, safe=False):
            P.add("dve", lambda h: h.scalar_tensor_tensor(out=out_, in0=in0, scalar=sc, in1=in1, op0=op0, op1=op1), r, w, safe=safe)

        def cp(eng, out_, in_, r, w, safe=False):
            if eng == "act":
                P.add("act", lambda h: h.activation(out=out_, in_=in_, func=AF.Copy), r, w, safe=safe)
            else:
                P.add(eng, lambda h: h.tensor_copy(out=out_, in_=in_), r, w, safe=safe)

        def recip(out_, in_, r, w):
            P.add("dve", lambda h: h.reciprocal(out=out_, in_=in_), r, w)

        ident = A(g, "ident", [128, 128], BF16)
        modT = A(g, "modT", [128, 48, 3])
        sc1p = A(g, "sc1p", [128, 8, 3])
        sc2p = A(g, "sc2p", [128, 8, 3])
        g2bc = A(g, "g2bc", [128, NB, 1024])
        epsln = A(g, "epsln", [128, 1])
        epsrms = A(g, "epsrms", [128, 1])
        gm = ExitStack()
        RTm = A(gm, "RTm", [128, 128], BF16)
        g1bc = A(gm, "g1bc", [128, NB, 1024])
        lam_neg = A(gm, "lam_neg", [128, 1])
        subg = A(gm, "subg", [128, 128])
        cw = A(gm, "cw", [128, 4, 4])
        cb = A(gm, "cb", [128, 4])
        gb = A(gm, "gb", [128, 16])
        c1 = A(gm, "c1", [128, 8])
        c1x2 = A(gm, "c1x2", [128, 8])
        one_t = A(gm, "one_t", [128, 1])
        wg = A(gm, "wg", [128, 16, 128], BF16)
        Win = A(gm, "Win", [128, 8, 2560], BF16)
        wst = [None, None]
        wst_n = [0]

        P.add("dve", lambda h: h.memset(epsln[:], LN_EPS), (), ["epsln"])
        P.add("dve", lambda h: h.memset(epsrms[:], RMS_EPS), (), ["epsrms"])

        def load_cast(dst_fn, src_fn, nchunks, key, shape3, stg=None, skey="wst", engs=("pool", "dve", "pool", "act")):
            stg = stg or wst
            for n in range(nchunks):
                sl = wst_n[0] % len(stg)
                wst_n[0] += 1
                st = stg[sl][:, 0:shape3[0] * shape3[1]].rearrange("p (a b) -> p a b", a=shape3[0])
                kk = skey[sl] if isinstance(skey, list) else (skey, sl)
                P.dma("sync", st, src_fn(n), (), [kk])
                eng = engs[n % len(engs)]
                cp(eng, dst_fn(n), st, [kk], [(key, n)])

        with ExitStack() as p0:
            wst[0] = A(p0, "wst0", [128, WST]); wst[1] = A(p0, "wst1", [128, WST])
            cTt = A(p0, "cTt", [128, 8, 3])
            scT = A(p0, "scT", [128, 8, 3])
            scb = A(p0, "scb", [128, 8, NB, 128])
            badT = A(p0, "badT", [128, 48])
            bst = [A(p0, "bst%d" % i, [128, 512]) for i in range(2)]
            lamt = A(p0, "lamt", [128, 2, 128])
            lprod = A(p0, "lprod", [128, 128])
            lq = A(p0, "lq", [128, 2])
            le = A(p0, "le", [128, 2])
            ltmp = A(p0, "ltmp", [128, 1])
            llam = A(p0, "llam", [128, 8])
            lz = A(p0, "lz", [128, 8]); lz2 = A(p0, "lz2", [128, 8]); lsr = A(p0, "lsr", [128, 8])
            wgst = A(p0, "wgst", [128, 16, 128])
            psmod = PS(p0, "psmod", [128, 48, 3])
            psg = [PS(p0, "psg%d" % i, [128, 512]) for i in range(2)]

            P.dma("sync", ident[:], ident_d[:, :], (), ["ident"])
            P.dma("sync", RTm[:], RT_d[:, :], (), ["RTm"])
            P.dma("sync", cTt[:], cT[:, :, :], (), ["cTt"])
            P.dma("sync", badT[:], b_adaT[:, :], (), ["badT"])
            P.dma("sync", lamt[:, 0, :], lamqk[0:1, :].to_broadcast([128, 128]), (), ["lamt0"])
            P.dma("sync", lamt[:, 1, :], lamqk[1:2, :].to_broadcast([128, 128]), (), ["lamt1"])
            P.dma("sync", subg[:], subg_d[0:1, :].to_broadcast([128, 128]), (), ["subg"])
            P.dma("sync", cw[:], conv_wT[:, :, :], (), ["cw"])
            P.dma("sync", cb[:], conv_bT[:, :], (), ["cb"])
            P.dma("sync", gb[:], gate_bT[:, :], (), ["gb"])
            P.dma("sync", llam[:], lru_lamT[:, :], (), ["llam"])
            P.dma("sync", wgst[:], wg_bd.rearrange("t p n -> p t n"), (), ["wgst"])
            cp("pool", wg[:], wgst[:], ["wgst"], ["wg"])
            act(scT[:], cTt[:], AF.Sigmoid, ["cTt"], ["scT"])
            tt("dve", scT[:], scT[:], cTt[:], ALU.mult, ["scT", "cTt"], ["scT"])
            for k in range(8):
                for b in range(NB):
                    cp("dve", scb[:, k, b, :], scT[:, k, b:b + 1].to_broadcast([128, 128]), ["scT"], [("scb", k, b)])
            tt("dve", lprod[:], lamt[:, 0, :], lamt[:, 1, :], ALU.mult, ["lamt0", "lamt1"], ["lprod"])
            P.add("dve", lambda h: h.tensor_reduce(out=lq[:], in_=lprod[:].rearrange("p (a b) -> p a b", a=2),
                                                  axis=AX.X, op=ALU.add), ["lprod"], ["lq"])
            act(le[:], lq[:], AF.Exp, ["lq"], ["le"])
            tt("dve", ltmp[:], le[:, 1:2], le[:, 0:1], ALU.subtract, ["le"], ["ltmp"])
            ts("dve", lam_neg[:], ltmp[:], -0.2, None, ALU.add, None, ["ltmp"], ["lam_neg"])
            ts("dve", subg[:], subg[:], 0.8, None, ALU.mult, None, ["subg"], ["subg"])
            act(lz[:], llam[:], AF.Exp, ["llam"], ["lz"], scale=-1.0)
            ts("dve", lz2[:], lz[:], 2.0, None, ALU.add, None, ["lz"], ["lz2"])
            recip(lz2[:], lz2[:], ["lz2"], ["lz2"])
            tt("dve", lz[:], lz[:], lz2[:], ALU.mult, ["lz", "lz2"], ["lz"])
            tt("dve", lz2[:], lz[:], lz[:], ALU.mult, ["lz"], ["lz2"])
            ts("dve", lsr[:], lz2[:], 1.0 / 11.0, 1.0 / 9.0, ALU.mult, ALU.add, ["lz2"], ["lsr"])
            for cf in (1.0 / 7.0, 1.0 / 5.0, 1.0 / 3.0, 1.0):
                tt("dve", lsr[:], lsr[:], lz2[:], ALU.mult, ["lsr", "lz2"], ["lsr"])
                ts("dve", lsr[:], lsr[:], cf, None, ALU.add, None, ["lsr"], ["lsr"])
            tt("dve", lsr[:], lsr[:], lz[:], ALU.mult, ["lsr", "lz"], ["lsr"])
            ts("dve", c1[:], lsr[:], -16.0, None, ALU.mult, None, ["lsr"], ["c1"])
            ts("dve", c1x2[:], lsr[:], -32.0, None, ALU.mult, None, ["lsr"], ["c1x2"])
            P.add("dve", lambda h: h.memset(one_t[:], 1.0), (), ["one_t"])

            for n in range(24):
                sl = wst_n[0] % 2
                wst_n[0] += 1
                wa = wst[sl][:, :].rearrange("p (k c) -> p k c", k=8)
                P.dma("sync", wa, w_ada_v[:, :, n * 256:(n + 1) * 256], (), [("wst", sl)])
                for j in range(2):
                    fc = 2 * n + j
                    if 16 <= fc < 24 or fc >= 40:
                        continue
                    for k in range(8):
                        mm(psmod[:, fc, :], wa[:, k, j * 128:(j + 1) * 128], scT[:, k, :], k == 0, k == 7,
                           [("wst", sl), "scT"], ["psmod"])
                gsel = None
                if 8 <= n < 12: gsel = (g1bc, (n - 8) * 256)
                if 20 <= n < 24: gsel = (g2bc, (n - 20) * 256)
                if gsel is not None:
                    for b in range(NB):
                        for k in range(8):
                            mm(psg[b][:, 0:256], scb[:, k, b, :], wa[:, k, :], k == 0, k == 7,
                               [("wst", sl)] + [("scb", k, b)], [("psg", b)])
                        P.dma("sync", bst[b][:, 0:256], b_ada[0:1, n * 256:(n + 1) * 256].to_broadcast([128, 256]), (), [("bst", b)])
                        tt("dve", gsel[0][:, b, gsel[1]:gsel[1] + 256], psg[b][:, 0:256], bst[b][:, 0:256], ALU.add,
                           [("psg", b), ("bst", b)], [("gbc", id(gsel[0]), b)])
            win_order = list(range(4, 16)) + [0, 1, 2, 3] + list(range(16, 20))
            load_cast(lambda n: Win[:, :, win_order[n] * 128:(win_order[n] + 1) * 128],
                      lambda n: w_in_v[:, :, win_order[n] * 128:(win_order[n] + 1) * 128], 20, "Win", (8, 128), engs=("dve", "act"))
            P.add("dve", lambda h: h.memset(modT[:], 0.0), (), ["modT"])
            for col in range(3):
                for lo_, hi_ in ((0, 16), (24, 40)):
                    tt("dve", modT[:, lo_:hi_, col], psmod[:, lo_:hi_, col], badT[:, lo_:hi_], ALU.add, ["psmod", "badT"], ["modT"])
            ts("dve", sc1p[:], modT[:, 8:16, :], 1.0, None, ALU.add, None, ["modT"], ["sc1p"])
            ts("dve", sc2p[:], modT[:, 32:40, :], 1.0, None, ALU.add, None, ["modT"], ["sc2p"])
            if dbg:
                P.dma("pool", dbg_out["d_mod"][:, :], modT[:].rearrange("p a b -> p (a b)"), ["modT"], [("dbg", "mod")])
            P.barrier()
            P.flush()

        def layernorm_stats(xt, st6, mv, rstd, eps_t, rk, key):
            P.add("dve", lambda h: h.bn_stats(out=st6[:, 0, :], in_=xt[:, 0:512]), rk, [(key, "st0")])
            P.add("dve", lambda h: h.bn_stats(out=st6[:, 1, :], in_=xt[:, 512:1024]), rk, [(key, "st1")])
            P.add("dve", lambda h: h.bn_aggr(out=mv[:], in_=st6[:].rearrange("p a b -> p (a b)")),
                  [(key, "st0"), (key, "st1")], [(key, "mv")])
            act(rstd[:], mv[:, 1:2], AF.Sqrt, [(key, "mv"), "epsln"], [(key, "rstd")], bias=eps_t[:])
            recip(rstd[:], rstd[:], [(key, "rstd")], [(key, "rstd")])

        L = {}

        def lru_alloc(es):
            for nm, shp, dt_ in (("ub", [128, 516], F32), ("ucvb", [128, 512], BF16), ("ra", [128, 512], F32),
                                 ("hsc", [128, 512], F32), ("yst", [128, 512], BF16), ("carry", [128, 1], F32)):
                L[nm] = [A(es, "%s%d" % (nm, i), shp, dt_) for i in range(4)]
            for nm in ("hfb", "gg", "ucv", "av", "ib", "m2"):
                L[nm] = [[A(es, "%s%d_%d" % (nm, i, j), [128, 512], F32) for j in range(2)] for i in range(4)]
            L["pz"] = [PS(es, "pz%d" % i, [128, 512]) for i in range(8)]

        def lru_pass(b, d):
            ub, ucv, ucvb, ra, av, ib, m2, hsc, hfb, gg, yst, carry, pz = (L[k] for k in (
                "ub", "ucv", "ucvb", "ra", "av", "ib", "m2", "hsc", "hfb", "gg", "yst", "carry", "pz"))
            order = list(range(9)) if d == 0 else [0] + list(range(8, 0, -1))
            NS = len(order)
            CH = range(4)

            def seginfo(seg):
                n = 256 if seg == 0 else 512
                t0 = 0 if seg == 0 else CL + (seg - 1) * 512
                qlo, qhi = (0, CL) if seg == 0 else (CL, T)
                return n, t0, (seg - 1) * 512, max(qlo, t0 - 2), min(qhi, t0 + n + 1)

            def loads(si):
                seg = order[si]
                n, t0, s0, lo, hi = seginfo(seg)
                for c in CH:
                    kub = ("ub", c)
                    if lo > t0 - 2:
                        P.add("pool", lambda h, c=c: h.memset(ub[c][:, 0:2], 0.0), (), [kub])
                    if hi < t0 + n + 1:
                        P.add("pool", lambda h, c=c, n=n: h.memset(ub[c][:, n + 2:n + 3], 0.0), (), [kub])
                    P.dma("sync", ub[c][:, lo - (t0 - 2):hi - (t0 - 2)], uT_s[b, c, :, lo:hi],
                          [("uT_s", b, c, g_) for g_ in (seg - 1, seg, seg + 1)], [kub])

            def loads_hg(si):
                seg = order[si]
                n, t0, s0, lo, hi = seginfo(seg)
                for c in CH:
                    if d == 1 and seg > 0:
                        P.dma("sync", hfb[c][si % 2][:], hf_s[b, c, :, s0:s0 + 512], [("hf_s", b, c, seg)], [("hfb", c, si % 2)])
                        P.dma("sync", gg[c][si % 2][:], gT_s[b, c, :, s0:s0 + 512], [("gT_s", b, c, seg)], [("gg", c, si % 2)])

            def conv(si):
                n = seginfo(order[si])[0]
                z = si % 2
                for c in CH:
                    ts("dve", ucv[c][z][:, 0:n], ub[c][:, 0:n], cw[:, c, 0:1], cb[:, c:c + 1], ALU.mult, ALU.add,
                       [("ub", c), "cw", "cb"], [("ucv", c, z)])
                for j in range(1, 4):
                    for c in CH:
                        stt(ucv[c][z][:, 0:n], ub[c][:, j:j + n], cw[:, c, j:j + 1], ucv[c][z][:, 0:n], ALU.mult, ALU.add,
                            [("ub", c), "cw"], [("ucv", c, z)])

            def stage_b(si):
                n = seginfo(order[si])[0]
                z = si % 2
                for c in CH:
                    cp("act", ucvb[c][:, 0:n], ucv[c][z][:, 0:n], [("ucv", c, z)], [("ucvb", c)])
                for c in CH:
                    gr = (d * 2 + 0) * 4 + c; gi_ = (d * 2 + 1) * 4 + c
                    mm(pz[2 * c][:, 0:n], wg[:, gr, :], ucvb[c][:, 0:n], True, True, [("ucvb", c), "wg"], [("pz", 2 * c)])
                    mm(pz[2 * c + 1][:, 0:n], wg[:, gi_, :], ucvb[c][:, 0:n], True, True, [("ucvb", c), "wg"], [("pz", 2 * c + 1)])
                for c in CH:
                    gr = (d * 2 + 0) * 4 + c; gi_ = (d * 2 + 1) * 4 + c
                    act(ra[c][:, 0:n], pz[2 * c][:, 0:n], AF.Sigmoid, ["gb"], [("pz", 2 * c), ("ra", c)], bias=gb[:, gr:gr + 1])
                    act(ib[c][z][:, 0:n], pz[2 * c + 1][:, 0:n], AF.Sigmoid, ["gb"], [("pz", 2 * c + 1), ("ib", c, z)], bias=gb[:, gi_:gi_ + 1])
                for c in CH:
                    act(av[c][z][:, 0:n], ra[c][:, 0:n], AF.Exp, [("ra", c), "c1"], [("av", c, z)], scale=c1[:, d * 4 + c:d * 4 + c + 1])
                    act(m2[c][z][:, 0:n], ra[c][:, 0:n], AF.Exp, [("ra", c), "c1x2"], [("m2", c, z)], scale=c1x2[:, d * 4 + c:d * 4 + c + 1])
                for c in CH:
                    tt("pool", ib[c][z][:, 0:n], ib[c][z][:, 0:n], ucv[c][z][:, 0:n], ALU.mult, [("ucv", c, z), ("ib", c, z)], [("ib", c, z)])
                for c in CH:
                    act(m2[c][z][:, 0:n], m2[c][z][:, 0:n], AF.Sqrt, [("m2", c, z), "one_t"], [("m2", c, z)], bias=one_t[:, 0:1], scale=-1.0)

            def stage_c(si):
                seg = order[si]
                n, t0, s0, lo, hi = seginfo(seg)
                z = si % 2
                for c in CH:
                    tt("dve", ib[c][z][:, 0:n], ib[c][z][:, 0:n], m2[c][z][:, 0:n], ALU.mult, [("m2", c, z), ("ib", c, z)], [("ib", c, z)])
                for c in CH:
                    o_ap = hsc[c][:, 0:n]; a_ap = av[c][z][:, 0:n]; b_ap = ib[c][z][:, 0:n]
                    if d == 1:
                        o_ap = o_ap[:, ::-1]; a_ap = a_ap[:, ::-1]; b_ap = b_ap[:, ::-1]
                    init = 0.0 if si == 0 else carry[c][:, 0:1]
                    P.add("dve", lambda h, o_ap=o_ap, a_ap=a_ap, b_ap=b_ap, init=init: h.tensor_tensor_scan(
                        out=o_ap, data0=a_ap, data1=b_ap, initial=init, op0=ALU.mult, op1=ALU.add),
                        [("av", c, z), ("ib", c, z), ("carry", c)], [("hsc", c)])
                for c in CH:
                    last_col = hsc[c][:, n - 1:n] if d == 0 else hsc[c][:, 0:1]
                    cp("dve", carry[c][:, 0:1], last_col, [("hsc", c)], [("carry", c)])
                if seg == 0:
                    return
                if d == 0:
                    for c in CH:
                        P.dma_defer("sync", hf_s[b, c, :, s0:s0 + 512], hsc[c][:, :], [("hsc", c)], [("hf_s", b, c, seg)])
                else:
                    G_ = [gg[c][z] for c in CH]; H_ = [hfb[c][z] for c in CH]
                    W_ = [m2[c][z] for c in CH]; S_ = [av[c][z] for c in CH]
                    for c in CH:
                        act(W_[c][:], G_[c][:], AF.Square, [("gg", c, z), ("m2", c, z)], [("m2", c, z)], scale=0.21145921128196543)
                    for c in CH:
                        stt(W_[c][:], W_[c][:], 1.0, G_[c][:], ALU.add, ALU.mult, [("m2", c, z), ("gg", c, z)], [("m2", c, z)])
                    for c in CH:
                        act(S_[c][:], W_[c][:], AF.Sigmoid, [("m2", c, z), ("av", c, z)], [("av", c, z)], scale=1.5957691216057308)
                    for c in CH:
                        tt("pool", H_[c][:], H_[c][:], hsc[c][:], ALU.add, [("hsc", c), ("hfb", c, z)], [("hfb", c, z)])
                    for c in CH:
                        tt("pool", S_[c][:], S_[c][:], G_[c][:], ALU.mult, [("gg", c, z), ("av", c, z)], [("av", c, z)])
                    for c in CH:
                        tt("dve", yst[c][:], H_[c][:], S_[c][:], ALU.mult, [("hfb", c, z), ("av", c, z)], [("yst", c)])
                    for c in CH:
                        P.dma_defer("sync", yT_s[b, :, c, s0:s0 + 512], yst[c][:], [("yst", c)], [("yT_s", b, c, seg)])

            loads(0)
            loads_hg(0)
            conv(0)
            if NS > 1:
                loads(1)
            for si in range(NS):
                if si + 1 < NS:
                    conv(si + 1)
                    if si + 2 < NS:
                        loads(si + 2)
                    loads_hg(si + 1)
                P.undefer()
                stage_b(si)
                stage_c(si)
            P.undefer()

        for b in range(NB):
            if stage < 1:
                break
            with ExitStack() as p24:
                with ExitStack() as p12:
                    KT = A(p12, "KT", [128, 4, T], BF16)
                    V = A(p12, "V", [128, 34, 4, 129], BF16)
                    with ExitStack() as p1:
                        xin = [A(p1, "xin%d" % i, [128, 1024]) for i in range(2)]
                        st6 = [A(p1, "st6%d" % i, [128, 2, 6]) for i in range(2)]
                        mv = [A(p1, "mv%d" % i, [128, 2]) for i in range(2)]
                        rstd = [A(p1, "rstd%d" % i, [128, 1]) for i in range(2)]
                        xn = [A(p1, "xn%d" % i, [128, 1024], BF16) for i in range(4)]
                        if os.environ.get("K_SWAP"):
                            hT = [A(p1, "hT%d" % i, [128, 8, 512], BF16) for i in range(2)][::-1]
                        else:
                            hT = [A(p1, "hT%d" % i, [128, 8, 512], BF16) for i in range(2)]
                        qkbf = [A(p1, "qkbf%d" % i, [128, 512], BF16) for i in range(2)]
                        qf = [A(p1, "qf%d" % i, [128, 512]) for i in range(2)]
                        t1 = [A(p1, "t1%d" % i, [128, 512]) for i in range(2)]
                        t2 = [A(p1, "t2%d" % i, [128, 512]) for i in range(2)]
                        cst = [A(p1, "cst%d" % i, [128, 512]) for i in range(2)]
                        snt = [A(p1, "snt%d" % i, [128, 512]) for i in range(2)]
                        ugst = [A(p1, "ugst%d" % i, [128, 512]) for i in range(4)]
                        qst = [A(p1, "qst%d" % i, [128, 4, 512], BF16) for i in range(2)]
                        pst = [PS(p1, "pst%d" % i, [128, 8, 128], BF16) for i in range(2)]
                        pm = [PS(p1, "pm%d" % i, [128, 512]) for i in range(4)]
                        prot = [PS(p1, "prot%d" % i, [128, 512]) for i in range(2)]

                        win_keys = []
                        P.add("pool", lambda h: h.memset(V[:, :, :, 128:129], 1.0), (), [("Vone", b)])
                        cnt1 = {"t": 0, "pm": 0, "rc": 0, "ug": 0}

                        def ginfo(gi):
                            ntile = 2 if gi == 0 else 4
                            return (ntile, ntile * 128, gi % 2, (2 if gi == 0 else b),
                                    (0 if gi == 0 else CL + (gi - 1) * 512), (gi - 1) * 512)

                        def lnA(gi, tiles=None):
                            ntile, ntok, hs, col, tok0, s0 = ginfo(gi)
                            for t in (range(ntile) if tiles is None else tiles):
                                xs = cnt1["t"] % 2; cnt1["t"] += 1
                                kx = ("x", b, xs)
                                src = ctx[b, t * 128:(t + 1) * 128, :] if gi == 0 else x[b, s0 + t * 128:s0 + (t + 1) * 128, :]
                                P.dma("sync", xin[xs][:], src, (), [kx])
                                if t == ntile - 1:
                                    if gi > 0:
                                        P.dma("sync", cst[gi % 2][:], cosT[:, s0:s0 + 512], (), [("cst", gi % 2)])
                                        P.dma("sync", snt[gi % 2][:], sinT[:, s0:s0 + 512], (), [("snt", gi % 2)])
                                    P.undefer()
                                layernorm_stats(xin[xs], st6[xs], mv[xs], rstd[xs], epsln, [kx], ("ln", xs))
                                ts("dve", xn[t][:], xin[xs][:], mv[xs][:, 0:1], rstd[xs][:, 0:1], ALU.subtract, ALU.mult,
                                   [kx, (("ln", xs), "mv"), (("ln", xs), "rstd")], [("xn", t)])

                        def lnB(gi):
                            ntile, ntok, hs, col, tok0, s0 = ginfo(gi)
                            for t in range(ntile):
                                xs = t % 2
                                for k in range(8):
                                    tr(pst[xs][:, k, :], xn[t][:, k * 128:(k + 1) * 128], [("xn", t), "ident"], [("pst", xs)])
                                for k in range(8):
                                    if xs == 0:
                                        act(hT[hs][:, k, t * 128:(t + 1) * 128], pst[xs][:, k, :], AF.Identity,
                                            ["sc1p", "modT"], [("pst", xs), ("hT", hs, t)],
                                            bias=modT[:, k, col:col + 1], scale=sc1p[:, k, col:col + 1], safe=True)
                                    else:
                                        ts("dve", hT[hs][:, k, t * 128:(t + 1) * 128], pst[xs][:, k, :], sc1p[:, k, col:col + 1],
                                           modT[:, k, col:col + 1], ALU.mult, ALU.add,
                                           ["sc1p", "modT"], [("pst", xs), ("hT", hs, t)], safe=True)

                        def proj_mm(gi, fc):
                            ntile, ntok, hs, col, tok0, s0 = ginfo(gi)
                            hkeys = [("hT", hs, t) for t in range(ntile)]
                            pb = cnt1["pm"] % 4; cnt1["pm"] += 1
                            for k in range(8):
                                mm(pm[pb][:, 0:ntok], Win[:, k, fc * 128:(fc + 1) * 128], hT[hs][:, k, 0:ntok], k == 0, k == 7,
                                   hkeys + win_keys, [("pm", pb)])
                            return pb

                        def rope_tail(gi, fc, rs):
                            ntile, ntok, hs, col, tok0, s0 = ginfo(gi)
                            qs_ = gi % 2; cs_ = gi % 2
                            mm(prot[rs][:, :], RTm[:], qkbf[rs][:], True, True, [("qkbf", rs), "RTm"], [("prot", rs)])
                            tt("pool", t1[rs][:], qf[rs][:], cst[cs_][:], ALU.mult, [("qf", rs), ("cst", cs_)], [("t1", rs)])
                            tt("dve", t2[rs][:], prot[rs][:, :], snt[cs_][:], ALU.mult, [("snt", cs_)], [("prot", rs), ("t2", rs)])
                            if fc < 4:
                                tt("pool", qst[qs_][:, fc, :], t1[rs][:], t2[rs][:], ALU.add, [("t1", rs), ("t2", rs)], [("qst", qs_, fc)])
                            else:
                                tt("pool", KT[:, fc - 4, tok0:tok0 + 512], t1[rs][:], t2[rs][:], ALU.add,
                                   [("t1", rs), ("t2", rs)], [("KT", b, fc - 4, gi)])

                        def proj_qk(gi):
                            ntile, ntok, hs, col, tok0, s0 = ginfo(gi)
                            pend = None
                            chunks = [4, 5, 6, 7] if gi == 0 else list(range(8))
                            for idx, fc in enumerate(chunks):
                                if gi + 1 < 9:
                                    if gi == 0:
                                        lnA(gi + 1, [idx])
                                    elif idx % 2 == 0:
                                        lnA(gi + 1, [idx // 2])
                                pb = proj_mm(gi, fc)
                                if gi == 0:
                                    cp("act", KT[:, fc - 4, 0:ntok], pm[pb][:, 0:ntok], [], [("pm", pb), ("KT", b, fc - 4, gi)])
                                else:
                                    rs = cnt1["rc"] % 2; cnt1["rc"] += 1
                                    cp("act", qf[rs][:], pm[pb][:, :], [], [("pm", pb), ("qf", rs)])
                                    cp("dve", qkbf[rs][:], qf[rs][:], [("qf", rs)], [("qkbf", rs)])
                                    if pend is not None:
                                        rope_tail(gi, *pend)
                                    pend = (fc, rs)
                            if pend is not None:
                                rope_tail(gi, *pend)
                            if gi > 0:
                                P.dma_defer("sync", qT_s[b, :, :, s0:s0 + 512], qst[gi % 2][:], [("qst", gi % 2, f) for f in range(4)],
                                            [("qT_s", b, gi - 1)])

                        def proj_vug(gi):
                            ntile, ntok, hs, col, tok0, s0 = ginfo(gi)
                            hkeys = [("hT", hs, t) for t in range(ntile)]
                            for t in range(ntile):
                                pb = cnt1["pm"] % 4; cnt1["pm"] += 1
                                for k in range(8):
                                    mm(pm[pb][:, :], hT[hs][:, k, t * 128:(t + 1) * 128], Win[:, k, 1024:1536], k == 0, k == 7,
                                       hkeys + win_keys, [("pm", pb)])
                                vt = tok0 // 128 + t
                                cp("dve" if t % 2 else "act", V[:, vt, :, 0:128], pm[pb][:, :].rearrange("p (h d) -> p h d", h=4),
                                   [], [("pm", pb), ("V", b, vt)])
                            for fc in range(12, 20):
                                if gi == 0 and fc >= 16:
                                    continue
                                pb = proj_mm(gi, fc)
                                us = cnt1["ug"] % 4; cnt1["ug"] += 1
                                cp("act" if fc % 2 else "dve", ugst[us][:, 0:ntok], pm[pb][:, 0:ntok], [], [("pm", pb), ("ugst", us)])
                                if fc < 16:
                                    P.dma_defer("sync", uT_s[b, fc - 12, :, tok0:tok0 + ntok], ugst[us][:, 0:ntok], [("ugst", us)],
                                                [("uT_s", b, fc - 12, gi)])
                                else:
                                    P.dma_defer("sync", gT_s[b, fc - 16, :, s0:s0 + 512], ugst[us][:, :], [("ugst", us)],
                                                [("gT_s", b, fc - 16, gi)])
                                if cnt1["ug"] % 4 == 0:
                                    P.undefer()

                        lnA(0)
                        lnB(0)
                        for gi in range(9):
                            proj_qk(gi)
                            if gi + 1 < 9:
                                lnB(gi + 1)
                            proj_vug(gi)
                        P.undefer()
                        if dbg and b == 0:
                            P.dma("pool", dbg_out["d_k"][:, :], KT[:].rearrange("p a b -> p (a b)"),
                                  [("KT", b, h_, gi_) for h_ in range(4) for gi_ in range(9)], [("dbg", "k")])
                            P.dma("pool", dbg_out["d_v"][:, :], V[:].rearrange("p a b c -> p (a b c)"),
                                  [("V", b, vt_) for vt_ in range(34)] + [("Vone", b)], [("dbg", "v")])
                        P.barrier()
                        P.flush()
                    if stage < 2:
                        continue
                    with ExitStack() as p2:
                        qblk = [A(p2, "qblk%d" % i, [128, 4, 512], BF16) for i in range(2)]
                        PT = [A(p2, "PT%d" % i, [128, 1024], BF16) for i in range(4)]
                        rl = A(p2, "rl", [128, 8]); nl = A(p2, "nl", [128, 4])
                        o1 = A(p2, "o1", [128, 4, 128]); od = A(p2, "od", [128, 4, 128]); sq = A(p2, "sq", [128, 4, 128])
                        ss = A(p2, "ss", [128, 4]); rr = A(p2, "rr", [128, 4])
                        onb = A(p2, "onb", [128, 4, 128], BF16)
                        ost = [A(p2, "ost%d" % i, [128, 512], BF16) for i in range(2)]
                        pss = [PS(p2, "pss%d" % i, [128, 1024]) for i in range(2)]
                        pso = [PS(p2, "pso%d" % i, [128, 512]) for i in range(3)]
                        ptr = PS(p2, "ptr", [128, 4, 128], BF16)
                        NKT = T // 128
                        ptc = 0; rnd = 0
                        NQB = int(os.environ.get("K_QB", "8"))
                        for QB in range(NQB):
                            qs_ = QB % 2
                            if QB == 0:
                                P.dma("sync", qblk[qs_][:], qT_s[b, :, :, QB * 512:(QB + 1) * 512], [("qT_s", b, QB)], [("qblk", qs_)])
                            for hh in range(4):
                                if hh == 1 and QB + 1 < NQB:
                                    P.dma("sync", qblk[1 - qs_][:], qT_s[b, :, :, (QB + 1) * 512:(QB + 2) * 512],
                                          [("qT_s", b, QB + 1)], [("qblk", 1 - qs_)])
                                def S_(kt):
                                    sb = kt % 2
                                    ksl = slice(kt * 128, (kt + 1) * 128)
                                    mm(pss[sb][:, 0:512], KT[0:64, hh, ksl], qblk[qs_][0:64, hh, :], True, True,
                                       [("qblk", qs_)], [("pss", sb)])
                                    mm(pss[sb][:, 512:1024], KT[64:128, hh, ksl], qblk[qs_][64:128, hh, :], True, True,
                                       [("qblk", qs_)], [("pss", sb)])
                                def PV_(kt, pt):
                                    for m in range(2):
                                        for qs in range(4):
                                            a = m * 4 + qs
                                            bank = a // 3; c0 = (a % 3) * 160
                                            mm(pso[bank][:, c0:c0 + 129], PT[pt][:, m * 512 + qs * 128:m * 512 + (qs + 1) * 128],
                                               V[:, kt, hh, :], (kt == 0 and a % 3 == 0), kt == NKT - 1,
                                               [("PT", pt)], [("pso", bank)], skip=True)
                                S_(0)
                                prev = None
                                for kt in range(NKT):
                                    if kt + 1 < NKT:
                                        S_(kt + 1)
                                    sb = kt % 2
                                    pt = ptc % 4; ptc += 1
                                    act(PT[pt][:], pss[sb][:, :], AF.Exp, [], [("pss", sb), ("PT", pt)], scale=0.125, safe=True)
                                    if prev is not None:
                                        PV_(*prev)
                                    prev = (kt, pt)
                                PV_(*prev)
                                def acc(a):
                                    return pso[a // 3][:, (a % 3) * 160:(a % 3) * 160 + 128]
                                for bank in range(3):
                                    n = 3 if bank < 2 else 2
                                    recip(rl[:, bank * 3:bank * 3 + n], pso[bank][:, 128:128 + 160 * (n - 1) + 1:160],
                                          [], [("pso", bank), ("rl", bank)])
                                ts("dve", nl[:], rl[:, 4:8], lam_neg[:, 0:1], None, ALU.mult, None,
                                   [("rl", 1), ("rl", 2), "lam_neg"], ["nl"])
                                for qs in range(4):
                                    ts("dve", o1[:, qs, :], acc(qs), rl[:, qs:qs + 1], None, ALU.mult, None,
                                       [("rl", 0), ("rl", 1)], [("pso", qs // 3), ("o1", qs)])
                                for qs in range(4):
                                    a = 4 + qs
                                    stt(od[:, qs, :], acc(a), nl[:, qs:qs + 1], o1[:, qs, :], ALU.mult, ALU.add,
                                        ["nl", ("o1", qs)], [("pso", a // 3), ("od", qs)])
                                tt("dve", sq[:], od[:], od[:], ALU.mult, [("od", q_) for q_ in range(4)], ["sq"])
                                P.add("dve", lambda h: h.tensor_reduce(out=ss[:], in_=sq[:], axis=AX.X, op=ALU.add), ["sq"], ["ss"])
                                act(rr[:], ss[:], AF.Ln, ["ss", "epsrms"], ["rr"], bias=epsrms[:], scale=1.0 / 128.0)
                                act(rr[:], rr[:], AF.Exp, ["rr"], ["rr"], scale=-0.5)
                                for qs in range(4):
                                    stt(onb[:, qs, :], od[:, qs, :], rr[:, qs:qs + 1], subg[:], ALU.mult, ALU.mult,
                                        ["rr", ("od", qs), "subg"], [("onb", qs)])
                                for qs in range(4):
                                    tr(ptr[:, qs, :], onb[:, qs, :], [("onb", qs), "ident"], ["ptr"])
                                os_ = rnd % 2; rnd += 1
                                cp("dve", ost[os_][:], ptr[:].rearrange("p a b -> p (a b)"), [], ["ptr", ("ost", os_)])
                                P.dma("sync", oT_s[b, :, hh, QB * 512:(QB + 1) * 512], ost[os_][:], [("ost", os_)],
                                      [("oT_s", b, hh, QB)])
                        if dbg and b == 0:
                            nq_ = int(os.environ.get("K_QB", "8")) * 512
                            P.dma("sync", dbg_out["d_o"].rearrange("p (a b) -> p a b", a=4)[:, :, 0:nq_], oT_s[b, :, :, 0:nq_],
                                  [("oT_s", b, h_, q_) for h_ in range(4) for q_ in range(8)], [("dbg", "o")])
                        P.barrier()
                        P.flush()
                if stage < 3:
                    continue
                with ExitStack() as p3:
                    lru_alloc(p3)
                    for d in range(2):
                        lru_pass(b, d)
                    if dbg and b == 0:
                        P.dma("sync", dbg_out["d_y"].rearrange("p (a b) -> p a b", a=4), yT_s[b, :, :, :],
                              [("yT_s", b, c_, s_) for c_ in range(4) for s_ in range(1, 9)], [("dbg", "y")])
                    P.barrier()
                    P.flush()
                if stage < 4:
                    continue
                with ExitStack() as p4:
                    Wout = A(p4, "Wout", [128, 8, 1024], BF16)
                    ln1g = A(p4, "ln1g", [128, 1024]); ln1b = A(p4, "ln1b", [128, 1024])
                    oyT = [A(p4, "oyT%d" % i, [128, 8, 512], BF16) for i in range(2)]
                    wst[0] = A(p4, "wst0", [128, WST]); wst[1] = A(p4, "wst1", [128, WST])
                    xin4 = [A(p4, "xin4%d" % i, [128, 1024]) for i in range(4)]
                    tt4 = [A(p4, "tt4%d" % i, [128, 1024]) for i in range(8)]
                    st64 = [A(p4, "st64%d" % i, [128, 2, 6]) for i in range(4)]
                    mv4 = [A(p4, "mv4%d" % i, [128, 2]) for i in range(4)]
                    rstd4 = [A(p4, "rstd4%d" % i, [128, 1]) for i in range(4)]
                    nmr4 = [A(p4, "nmr4%d" % i, [128, 1]) for i in range(4)]
                    py = [PS(p4, "py%d" % i, [128, 1024]) for i in range(4)]
                    load_cast(lambda n: Wout[:, :, n * 256:(n + 1) * 256], lambda n: w_out_v[:, :, n * 256:(n + 1) * 256],
                              4, ("Wout", b), (8, 256), engs=("dve", "act"))
                    wout_keys = [(("Wout", b), n) for n in range(4)]
                    P.dma("sync", ln1g[:], ln1[0:1, :].to_broadcast([128, 1024]), (), ["ln1g"])
                    P.dma("sync", ln1b[:], ln1[1:2, :].to_broadcast([128, 1024]), (), ["ln1b"])
                    for G in range(8):
                        gs = G % 2
                        P.dma("sync", oyT[gs][:, 0:4, :], oT_s[b, :, :, G * 512:(G + 1) * 512],
                              [("oT_s", b, h_, G) for h_ in range(4)], [("oyT", gs, 0)])
                        P.dma("sync", oyT[gs][:, 4:8, :], yT_s[b, :, :, G * 512:(G + 1) * 512],
                              [("yT_s", b, c_, G + 1) for c_ in range(4)], [("oyT", gs, 1)])
                        TT = range(4)
                        for t in TT:
                            tok = G * 512 + t * 128
                            P.dma("sync", xin4[t][:], x[b, tok:tok + 128, :], (), [("x4", t)])
                        P.undefer()
                        for t in TT:
                            for half in range(2):
                                for k in range(8):
                                    mm(py[t][:, half * 512:(half + 1) * 512], oyT[gs][:, k, t * 128:(t + 1) * 128],
                                       Wout[:, k, half * 512:(half + 1) * 512], k == 0, k == 7,
                                       [("oyT", gs, 0), ("oyT", gs, 1)] + wout_keys, [("py", t)])
                        for t in TT:
                            tt("dve", tt4[t + 4 * gs][:], py[t][:, :], g1bc[:, b, :], ALU.mult, [("gbc", id(g1bc), b)], [("py", t), ("tt4", t + 4 * gs)])
                        for t in TT:
                            stt(tt4[t + 4 * gs][:], xin4[t][:], ALPHA, tt4[t + 4 * gs][:], ALU.mult, ALU.add, [("x4", t), ("tt4", t + 4 * gs)], [("tt4", t + 4 * gs)])
                        for t in TT:
                            P.add("dve", lambda h, t=t, gs=gs: h.bn_stats(out=st64[t][:, 0, :], in_=tt4[t + 4 * gs][:, 0:512]), [("tt4", t + 4 * gs)], [("st4a", t)])
                            P.add("dve", lambda h, t=t, gs=gs: h.bn_stats(out=st64[t][:, 1, :], in_=tt4[t + 4 * gs][:, 512:1024]), [("tt4", t + 4 * gs)], [("st4b", t)])
                        for t in TT:
                            P.add("dve", lambda h, t=t: h.bn_aggr(out=mv4[t][:], in_=st64[t][:].rearrange("p a b -> p (a b)")),
                                  [("st4a", t), ("st4b", t)], [("mv4", t)])
                        for t in TT:
                            act(rstd4[t][:], mv4[t][:, 1:2], AF.Sqrt, [("mv4", t), "epsln"], [("rstd4", t)], bias=epsln[:])
                        for t in TT:
                            recip(rstd4[t][:], rstd4[t][:], [("rstd4", t)], [("rstd4", t)])
                        for t in TT:
                            stt(nmr4[t][:], mv4[t][:, 0:1], -1.0, rstd4[t][:, 0:1], ALU.mult, ALU.mult,
                                [("mv4", t), ("rstd4", t)], [("nmr4", t)])
                        for t in TT:
                            act(tt4[t + 4 * gs][:], tt4[t + 4 * gs][:], AF.Identity, [("tt4", t + 4 * gs), ("nmr4", t), ("rstd4", t)], [("tt4", t + 4 * gs)],
                                bias=nmr4[t][:, 0:1], scale=rstd4[t][:, 0:1])
                        for t in TT:
                            tt("pool", tt4[t + 4 * gs][:], tt4[t + 4 * gs][:], ln1g[:], ALU.mult, [("tt4", t + 4 * gs), "ln1g"], [("tt4", t + 4 * gs)])
                        for t in TT:
                            tt("pool", tt4[t + 4 * gs][:], tt4[t + 4 * gs][:], ln1b[:], ALU.add, [("tt4", t + 4 * gs), "ln1b"], [("tt4", t + 4 * gs)])
                        for t in TT:
                            tok = G * 512 + t * 128
                            P.dma_defer("sync", x1_s[b * S + tok:b * S + tok + 128, :], tt4[t + 4 * gs][:], [("tt4", t + 4 * gs)], [("x1_s", b, G, t)])
                    P.undefer()
                    if dbg and b == 0:
                        P.dma("sync", dbg_out["d_x1"][:, :], x1_s[0:S, :], [("x1_s", b, G_, t_) for G_ in range(8) for t_ in range(4)],
                              [("dbg", "x1")])
                    P.barrier()
                    P.flush()
        P.barrier()
        P.flush()
        gm.close()
        if stage >= 5:
            with ExitStack() as p5:
                W1 = A(p5, "W1", [128, 8, 2 * FH], BF16)
                W2 = A(p5, "W2", [128, 22, 1024], BF16)
                ln2g = A(p5, "ln2g", [128, 1024]); ln2b = A(p5, "ln2b", [128, 1024])
                xin5 = [A(p5, "xin5%d" % i, [128, 1024]) for i in range(4)]
                tt5 = [A(p5, "tt5%d" % i, [128, 1024]) for i in range(2)]
                xn5 = [A(p5, "xn5%d" % i, [128, 1024], BF16) for i in range(2)]
                h2T = [A(p5, "h2T%d" % i, [128, 8, 256], BF16) for i in range(2)]
                sa = [A(p5, "sa%d" % i, [128, 256]) for i in range(4)]
                actT = A(p5, "actT", [128, 22, 256], BF16)
                st65 = [A(p5, "st65%d" % i, [128, 2, 6]) for i in range(2)]
                mv5 = [A(p5, "mv5%d" % i, [128, 2]) for i in range(2)]
                rstd5 = [A(p5, "rstd5%d" % i, [128, 1]) for i in range(2)]
                st65o = [A(p5, "st65o%d" % i, [128, 2, 6]) for i in range(2)]
                mv5o = [A(p5, "mv5o%d" % i, [128, 2]) for i in range(2)]
                rstd5o = [A(p5, "rstd5o%d" % i, [128, 1]) for i in range(2)]
                pst5 = [PS(p5, "pst5%d" % i, [128, 8, 128], BF16) for i in range(2)]
                pAB = [PS(p5, "pAB%d" % i, [128, 512]) for i in range(4)]
                pf = PS(p5, "pf", [128, 1024])
                stg5 = tt5 + xin5
                stg5k = [("tt5", 0), ("tt5", 1)] + [("x5", i) for i in range(4)]
                w1_order = [c_ for j_ in range(22) for c_ in (j_, 22 + j_)]
                load_cast(lambda n: W1[:, :, w1_order[n] * 128:(w1_order[n] + 1) * 128],
                          lambda n: w_f1_v[:, :, w1_order[n] * 128:(w1_order[n] + 1) * 128],
                          44, "W1o", (8, 128), stg=stg5, skey=stg5k, engs=("dve", "act"))
                load_cast(lambda n: W2[:, n:n + 1, :], lambda n: w_f2_v[:, n:n + 1, :], 22, "W2", (1, 1024), stg=stg5, skey=stg5k,
                          engs=("dve", "act"))
                w2_keys = []
                P.dma("sync", ln2g[:], ln2[0:1, :].to_broadcast([128, 1024]), (), ["ln2g"])
                P.dma("sync", ln2b[:], ln2[1:2, :].to_broadcast([128, 1024]), (), ["ln2b"])
                abc = [0]
                groups = [(b_, G_) for b_ in range(NB) for G_ in range(S // 256)]

                def lnt(gidx, part):
                    b_, G = groups[gidx]
                    hs = gidx % 2
                    for t in range(2):
                        xs = (gidx * 2 + t) % 4
                        ps_ = (gidx * 2 + t) % 2
                        tok = G * 256 + t * 128
                        kx = ("x5", xs)
                        if part == "B":
                            for k in range(8):
                                tr(pst5[ps_][:, k, :], xn5[ps_][:, k * 128:(k + 1) * 128], [("xn5", ps_), "ident"], [("pst5", ps_)])
                            for k in range(8):
                                if ps_ == 0:
                                    act(h2T[hs][:, k, t * 128:(t + 1) * 128], pst5[ps_][:, k, :], AF.Identity,
                                        ["sc2p", "modT"], [("pst5", ps_), ("h2T", hs, t)],
                                        bias=modT[:, 24 + k, b_:b_ + 1], scale=sc2p[:, k, b_:b_ + 1], safe=True)
                                else:
                                    ts("dve", h2T[hs][:, k, t * 128:(t + 1) * 128], pst5[ps_][:, k, :], sc2p[:, k, b_:b_ + 1],
                                       modT[:, 24 + k, b_:b_ + 1], ALU.mult, ALU.add,
                                       ["sc2p", "modT"], [("pst5", ps_), ("h2T", hs, t)], safe=True)
                            continue
                        P.dma("sync", xin5[xs][:], x1_s[b_ * S + tok:b_ * S + tok + 128, :],
                              [("x1_s", b_, tok // 512, (tok % 512) // 128)], [kx])
                        layernorm_stats(xin5[xs], st65[ps_], mv5[ps_], rstd5[ps_], epsln, [kx], ("ln5", ps_))
                        ts("dve", xn5[ps_][:], xin5[xs][:], mv5[ps_][:, 0:1], rstd5[ps_][:, 0:1], ALU.subtract, ALU.mult,
                           [kx, (("ln5", ps_), "mv"), (("ln5", ps_), "rstd")], [("xn5", ps_)])

                def ffn_in(gidx):
                    hs = gidx % 2
                    hk = [("h2T", hs, 0), ("h2T", hs, 1)]
                    for j in range(22):
                        if j == 8 and gidx + 1 < len(groups):
                            lnt(gidx + 1, "A")
                        ia = abc[0] % 4; abc[0] += 1
                        pa = pAB[ia][:, 0:256]; pb_ = pAB[ia][:, 256:512]
                        kb = ("pAB", ia)
                        for k in range(8):
                            mm(pa, W1[:, k, j * 128:(j + 1) * 128], h2T[hs][:, k, :], k == 0, k == 7, hk + [("W1o", 2 * j)], [kb])
                        for k in range(8):
                            mm(pb_, W1[:, k, FH + j * 128:FH + (j + 1) * 128], h2T[hs][:, k, :], k == 0, k == 7,
                               hk + [("W1o", 2 * j + 1)], [kb])
                        act(sa[ia][:], pa, AF.Sigmoid, [], [kb, ("sa", ia)])
                        tt("dve", sa[ia][:], sa[ia][:], pa, ALU.mult, [], [kb, ("sa", ia)])
                        tt("dve", actT[:, j, :], sa[ia][:], pb_, ALU.mult, [("sa", ia)], [kb, ("actT", j)])

                def ffn_out(gidx):
                    b_, G = groups[gidx]
                    akeys = [("actT", j) for j in range(22)]
                    for t in range(2):
                        xs = (gidx * 2 + t) % 4
                        ts_ = (gidx * 2 + t) % 2
                        tok = G * 256 + t * 128
                        for half in range(2):
                            for j in range(22):
                                mm(pf[:, half * 512:(half + 1) * 512], actT[:, j, t * 128:(t + 1) * 128],
                                   W2[:, j, half * 512:(half + 1) * 512], j == 0, j == 21, akeys + [("W2", j)], ["pf"])
                        kt5 = ("tt5", ts_); kx = ("x5", xs); kl = ("ln5o", ts_)
                        tt("dve", tt5[ts_][:], pf[:, :], g2bc[:, b_, :], ALU.mult, [("gbc", id(g2bc), b_)], ["pf", kt5])
                        stt(tt5[ts_][:], xin5[xs][:], ALPHA, tt5[ts_][:], ALU.mult, ALU.add, [kx, kt5], [kt5])
                        layernorm_stats(tt5[ts_], st65o[ts_], mv5o[ts_], rstd5o[ts_], epsln, [kt5], kl)
                        ts("dve", tt5[ts_][:], tt5[ts_][:], mv5o[ts_][:, 0:1], rstd5o[ts_][:, 0:1], ALU.subtract, ALU.mult,
                           [kt5, (kl, "mv"), (kl, "rstd")], [kt5])
                        tt("pool", tt5[ts_][:], tt5[ts_][:], ln2g[:], ALU.mult, [kt5, "ln2g"], [kt5])
                        tt("pool", tt5[ts_][:], tt5[ts_][:], ln2b[:], ALU.add, [kt5, "ln2b"], [kt5])
                        P.dma("sync", out[b_, tok:tok + 128, :], tt5[ts_][:], [kt5], [("out", b_, tok)])

                lnt(0, "A")
                lnt(0, "B")
                for gidx in range(len(groups)):
                    ffn_in(gidx)
                    if gidx + 1 < len(groups):
                        lnt(gidx + 1, "B")
                    ffn_out(gidx)
                P.barrier()
                P.flush()
        P.barrier()
        P.flush()
    return nc


def _prep_inputs(inp):
    f32 = np.float32
    x = np.ascontiguousarray(inp["x"], f32); c = np.asarray(inp["c"], f32); ctx = np.ascontiguousarray(inp["ctx"], f32)
    c_ctx = np.asarray(inp["c_ctx"], f32)
    common = {}
    common["w_ada"] = np.ascontiguousarray(inp["w_ada"][0], f32)
    ba = np.asarray(inp["b_ada"][0], f32)
    common["b_adaT"] = np.ascontiguousarray(ba.reshape(48, 128).T)
    common["b_ada"] = np.ascontiguousarray(ba.reshape(1, 6144))
    common["w_in"] = np.ascontiguousarray(inp["w_in"][0], f32)
    common["lamqk"] = np.ascontiguousarray(np.stack([inp["lam_q"][0].reshape(128), inp["lam_k"][0].reshape(128)]).astype(f32))
    common["subg"] = np.ascontiguousarray(inp["subln_g"][0].reshape(1, 128), f32)
    cwt = np.asarray(inp["conv_w"][0], f32)
    common["conv_wT"] = np.ascontiguousarray(cwt.reshape(4, 4, 128).transpose(2, 1, 0))
    common["conv_bT"] = np.ascontiguousarray(np.asarray(inp["conv_b"][0], f32).reshape(4, 128).T)
    gbv = np.asarray(inp["lru_b_gates"][0], f32).reshape(2, 2, 4, 128)
    common["gate_bT"] = np.ascontiguousarray(gbv.transpose(3, 0, 1, 2).reshape(128, 16))
    ll = np.asarray(inp["lru_lambda"][0], f32).reshape(2, 4, 128)
    common["lru_lamT"] = np.ascontiguousarray(ll.transpose(2, 0, 1).reshape(128, 8))
    wgt = np.asarray(inp["lru_w_gates"][0], f32)
    bd = np.zeros((2, 2, 4, 128, 128), f32)
    for cc in range(4):
        bd[:, :, cc, 0:64, 0:64] = wgt[:, :, 2 * cc]
        bd[:, :, cc, 64:128, 64:128] = wgt[:, :, 2 * cc + 1]
    common["wg_bd"] = np.ascontiguousarray(bd.reshape(16, 128, 128))
    common["w_out"] = np.ascontiguousarray(inp["w_out"][0], f32)
    common["ln1"] = np.ascontiguousarray(np.stack([inp["ln1_g"][0], inp["ln1_b"][0]]).astype(f32))
    common["w_f1"] = np.ascontiguousarray(inp["w_ffn_in"][0], f32)
    common["w_f2"] = np.ascontiguousarray(inp["w_ffn_out"][0], f32)
    common["ln2"] = np.ascontiguousarray(np.stack([inp["ln2_g"][0], inp["ln2_b"][0]]).astype(f32))
    common["ident"] = np.eye(128, dtype=f32).astype(ml_dtypes.bfloat16)
    RT = np.zeros((128, 128), f32)
    for fo in range(128):
        if fo % 32 < 16:
            RT[fo + 16, fo] = -1.0
        else:
            RT[fo - 16, fo] = 1.0
    common["RT"] = RT.astype(ml_dtypes.bfloat16)
    inv_freq = (np.float32(10000.0) ** (-np.arange(16, dtype=f32) / np.float32(16))).astype(f32)
    tok = np.arange(S)
    row = (tok // 64).astype(f32); colp = (tok % 64).astype(f32)
    ang_r = (row[None, :] * inv_freq[:, None]).astype(f32)
    ang_c = (colp[None, :] * inv_freq[:, None]).astype(f32)
    a64 = np.concatenate([ang_r, ang_r, ang_c, ang_c], axis=0)
    a128 = np.concatenate([a64, a64], axis=0)
    common["cosT"] = np.ascontiguousarray(np.cos(a128).astype(f32))
    common["sinT"] = np.ascontiguousarray(np.sin(a128).astype(f32))
    in_maps = []
    for i in range(NCORES):
        m = dict(common)
        m["x"] = x[NB * i:NB * (i + 1)]
        m["ctx"] = ctx[NB * i:NB * (i + 1)]
        cv = np.stack([c[NB * i], c[NB * i + 1], c_ctx], axis=1)
        m["cT"] = np.ascontiguousarray(cv.reshape(8, 128, 3).transpose(1, 0, 2))
        in_maps.append(m)
    return in_maps


def kernel(**inputs):
    in_maps = _prep_inputs(inputs)
    nc = build()
    res = run_bass_kernel_spmd(nc, in_maps, core_ids=list(range(NCORES)))
    outs = [np.asarray(r["out"], np.float32) for r in res.results]
    return np.concatenate(outs, axis=0)
```
